# Optimizing a Trainium2 kernel written in Bass

```python
import math
import jax, jax.numpy as jnp
from jax import lax
import numpy as np

D_MODEL = 2048
BATCH = 8
SEQ = 4096
DEPTH = 4
DEC_BATCH = 2
DEC_SEQ = 4096
PAST_LEN = 128

GRID_W = 64
NA_HEAD_DIM = 64
NA_HEADS = D_MODEL // (2 * NA_HEAD_DIM)
NA_WIN_ROWS = 8
NA_WIN_COLS = 16
NA_QBLK_COLS = 16
SW_HEAD_DIM = 64
SW_Q_HEADS = D_MODEL // (2 * SW_HEAD_DIM)
SW_KV_HEADS = SW_Q_HEADS // 4
SW_RADIUS = 128
DIL_PAIRS = ((128, 1), (512, 4), (2048, 16))
DIL_HEAD_DIM = 64
DIL_HEADS = D_MODEL // (2 * DIL_HEAD_DIM)
ROPE_THETA = 500000.0
ROPE_FRACTION = 4
FFN_HIDDEN = ((8 * D_MODEL + 3 * 256 - 1) // (3 * 256)) * 256
RMS_EPS = 1e-6
NEG_INF = -1e30

A_W = NA_HEADS * NA_HEAD_DIM
B_QW = SW_Q_HEADS * SW_HEAD_DIM
B_KVW = SW_KV_HEADS * SW_HEAD_DIM
AB_IN = 3 * A_W + B_QW + 2 * B_KVW
AB_OUT = A_W + B_QW
C_W = DIL_HEADS * DIL_HEAD_DIM
C_IN = len(DIL_PAIRS) * 3 * C_W
C_OUT = C_W

kernel_name = "hybrid_natten_swa_dilated_encoder"


def rms_norm(x, g):
    xf = x.astype(jnp.float32)
    y = xf * lax.rsqrt(jnp.mean(xf * xf, axis=-1, keepdims=True) + RMS_EPS)
    return (y * g.astype(jnp.float32)).astype(x.dtype)


def rotary(x, pos):
    rot = x.shape[-1] // ROPE_FRACTION
    half = rot // 2
    inv = jnp.exp(-math.log(ROPE_THETA) * jnp.arange(half, dtype=jnp.float32) * (2.0 / rot))
    ang = pos[:, None] * inv[None, :]
    cos = jnp.cos(ang)[:, None, :].astype(x.dtype)
    sin = jnp.sin(ang)[:, None, :].astype(x.dtype)
    x1 = x[..., :half]
    x2 = x[..., half:rot]
    return jnp.concatenate([x1 * cos - x2 * sin, x2 * cos + x1 * sin, x[..., rot:]], axis=-1)


def banded_attention(q, k, v, radius, sink=None):
    n, L, hq, dh = q.shape
    hk = k.shape[2]
    g = hq // hk
    blk = radius
    nb = -(-L // blk)
    lp = nb * blk
    pad = lp - L
    qb = jnp.pad(q, ((0, 0), (0, pad), (0, 0), (0, 0))).reshape(n, nb, blk, hk, g, dh)

    def key_windows(t):
        tp = jnp.pad(t, ((0, 0), (blk, pad + blk), (0, 0), (0, 0))).reshape(n, nb + 2, blk, hk, dh)
        return jnp.concatenate([tp[:, :-2], tp[:, 1:-1], tp[:, 2:]], axis=2)

    kw, vw = key_windows(k), key_windows(v)
    qpos = jnp.arange(lp).reshape(nb, blk)
    kpos = (jnp.arange(nb)[:, None] - 1) * blk + jnp.arange(3 * blk)[None, :]
    kp = kpos[:, None, :]
    mask = (jnp.abs(qpos[:, :, None] - kp) <= radius) & (kp >= 0) & (kp < L)
    s = jnp.einsum('nbikgd,nbukd->nbkgiu', qb, kw).astype(jnp.float32) * (dh ** -0.5)
    s = jnp.where(mask[None, :, None, None], s, NEG_INF)
    m = jnp.max(s, axis=-1)
    if sink is not None:
        sk = sink.astype(jnp.float32).reshape(hk, g)[None, None, :, :, None]
        m = jnp.maximum(m, sk)
    p = jnp.exp(s - m[..., None])
    den = jnp.sum(p, axis=-1)
    if sink is not None:
        den = den + jnp.exp(sk - m)
    o = jnp.einsum('nbkgiu,nbukd->nbikgd', (p / den[..., None]).astype(v.dtype), vw)
    lse = jnp.moveaxis(m + jnp.log(den), -1, 2)
    o = o.reshape(n, lp, hq, dh)[:, :L]
    lse = lse.reshape(n, lp, hq)[:, :L]
    return o, lse


def dilated_attention(q, k, v, window, dilation):
    b, L, h, dh = q.shape
    radius = window // (2 * dilation)
    lm = L // dilation

    def split(t):
        return t.reshape(b, lm, dilation, h, dh).transpose(0, 2, 1, 3, 4).reshape(b * dilation, lm, h, dh)

    o, lse = banded_attention(split(q), split(k), split(v), radius)
    o = o.reshape(b, dilation, lm, h, dh).transpose(0, 2, 1, 3, 4).reshape(b, L, h, dh)
    lse = lse.reshape(b, dilation, lm, h).transpose(0, 2, 1, 3).reshape(b, L, h)
    return o, lse


def neighbourhood_attention(q, k, v, rpb):
    b, L, h, dh = q.shape
    rows = L // GRID_W
    kh = min(NA_WIN_ROWS, rows)
    kw = NA_WIN_COLS
    qr = math.gcd(rows, NA_WIN_ROWS)
    qc = NA_QBLK_COLS
    kr = min(kh + qr - 1, rows)
    kc = min(kw + qc - 1, GRID_W)
    nrb, ncb = rows // qr, GRID_W // qc
    r0 = jnp.arange(nrb) * qr
    c0 = jnp.arange(ncb) * qc
    krow = jnp.clip(r0 - kh // 2, 0, rows - kr)[:, None] + jnp.arange(kr)[None, :]
    kcol = jnp.clip(c0 - kw // 2, 0, GRID_W - kc)[:, None] + jnp.arange(kc)[None, :]
    qrow = r0[:, None] + jnp.arange(qr)[None, :]
    qcol = c0[:, None] + jnp.arange(qc)[None, :]
    rs = jnp.clip(qrow - kh // 2, 0, rows - kh)[:, :, None]
    cs = jnp.clip(qcol - kw // 2, 0, GRID_W - kw)[:, :, None]
    row_ok = (krow[:, None, :] >= rs) & (krow[:, None, :] < rs + kh)
    col_ok = (kcol[:, None, :] >= cs) & (kcol[:, None, :] < cs + kw)
    dr = jnp.clip(krow[:, None, :] - qrow[:, :, None] + NA_WIN_ROWS - 1, 0, 2 * NA_WIN_ROWS - 2)
    dc = jnp.clip(kcol[:, None, :] - qcol[:, :, None] + NA_WIN_COLS - 1, 0, 2 * NA_WIN_COLS - 2)
    bias = rpb.astype(jnp.float32)[:, dr[:, None, :, None, :, None], dc[None, :, None, :, None, :]]
    mask = row_ok[:, None, :, None, :, None] & col_ok[None, :, None, :, None, :]
    bias = jnp.moveaxis(jnp.where(mask[None], bias, NEG_INF), 0, 2)

    qg = q.reshape(b, nrb, qr, ncb, qc, h, dh)

    def gather(t):
        t = t.reshape(b, rows, GRID_W, h, dh)
        t = jnp.take(t, krow, axis=1)
        return jnp.take(t, kcol, axis=3)

    kg, vg = gather(k), gather(v)
    s = jnp.einsum('bRiCjhd,bRuCwhd->bRChijuw', qg, kg).astype(jnp.float32) * (dh ** -0.5) + bias[None]
    p = jax.nn.softmax(s.reshape(s.shape[:-2] + (kr * kc,)), axis=-1).reshape(s.shape)
    o = jnp.einsum('bRChijuw,bRuCwhd->bRiCjhd', p.astype(v.dtype), vg)
    return o.reshape(b, L, h, dh)


def ab_mixer(h, w_in, w_out, rpb, sink, pos):
    b, L, _ = h.shape
    proj = jnp.einsum('bld,de->ble', h, w_in)
    o3 = 3 * A_W
    o4 = o3 + B_QW
    o5 = o4 + B_KVW
    qa = proj[..., :A_W].reshape(b, L, NA_HEADS, NA_HEAD_DIM)
    ka = proj[..., A_W:2 * A_W].reshape(b, L, NA_HEADS, NA_HEAD_DIM)
    va = proj[..., 2 * A_W:o3].reshape(b, L, NA_HEADS, NA_HEAD_DIM)
    qb = rotary(proj[..., o3:o4].reshape(b, L, SW_Q_HEADS, SW_HEAD_DIM), pos)
    kb = rotary(proj[..., o4:o5].reshape(b, L, SW_KV_HEADS, SW_HEAD_DIM), pos)
    vb = proj[..., o5:].reshape(b, L, SW_KV_HEADS, SW_HEAD_DIM)
    out_a = neighbourhood_attention(qa, ka, va, rpb)
    out_b, _ = banded_attention(qb, kb, vb, SW_RADIUS, sink)
    cat = jnp.concatenate([out_a.reshape(b, L, A_W), out_b.reshape(b, L, B_QW)], axis=-1)
    return jnp.einsum('ble,ed->bld', cat, w_out)


def c_mixer(h, w_in, w_out, pos):
    b, L, _ = h.shape
    proj = jnp.einsum('bld,de->ble', h, w_in).reshape(b, L, len(DIL_PAIRS), 3, DIL_HEADS, DIL_HEAD_DIM)
    outs, lses = [], []
    for gi, (win, dil) in enumerate(DIL_PAIRS):
        q = rotary(proj[:, :, gi, 0], pos)
        k = rotary(proj[:, :, gi, 1], pos)
        o, lse = dilated_attention(q, k, proj[:, :, gi, 2], win, dil)
        outs.append(o)
        lses.append(lse)
    wts = jax.nn.softmax(jnp.stack(lses, axis=0), axis=0)
    o = jnp.sum(wts[..., None] * jnp.stack(outs, axis=0).astype(jnp.float32), axis=0)
    return jnp.einsum('ble,ed->bld', o.astype(h.dtype).reshape(b, L, C_W), w_out)


def swiglu(h, wg, wu, wd):
    hid = jax.nn.silu(jnp.einsum('bld,df->blf', h, wg)) * jnp.einsum('bld,df->blf', h, wu)
    return jnp.einsum('blf,fd->bld', hid, wd)


def trunk(x, g_mix_pre, g_mix_post, g_ffn_pre, g_ffn_post, w_in_ab, w_out_ab, rpb_a,
          sink_b, w_in_c, w_out_c, w_gate, w_up, w_down):
    L = x.shape[1]
    pos = jnp.arange(L, dtype=jnp.float32)
    for layer in range(DEPTH):
        i = layer // 2
        hn = rms_norm(x, g_mix_pre[layer])
        if layer % 2 == 0:
            m = ab_mixer(hn, w_in_ab[i], w_out_ab[i], rpb_a[i], sink_b[i], pos)
        else:
            m = c_mixer(hn, w_in_c[i], w_out_c[i], pos)
        x = x + rms_norm(m, g_mix_post[layer])
        f = swiglu(rms_norm(x, g_ffn_pre[layer]), w_gate[layer], w_up[layer], w_down[layer])
        x = x + rms_norm(f, g_ffn_post[layer])
    return x


def setup_inputs(seed: int = 0) -> dict:
    key = jax.random.key(seed)
    ks = jax.random.split(key, 16)
    n_even = (DEPTH + 1) // 2
    n_odd = DEPTH // 2

    def w(k, shape, fan_in):
        return jax.random.normal(k, shape, jnp.float32) * (fan_in ** -0.5)

    def gain(k):
        return 1.0 + 0.05 * jax.random.normal(k, (DEPTH, D_MODEL), jnp.float32)

    return {
        "x_prompt": jax.random.normal(ks[0], (BATCH, SEQ, D_MODEL), jnp.float32),
        "x_sample": jax.random.normal(ks[1], (DEC_BATCH, DEC_SEQ, D_MODEL), jnp.float32),
        "g_mix_pre": gain(ks[2]),
        "g_mix_post": gain(ks[3]),
        "g_ffn_pre": gain(ks[4]),
        "g_ffn_post": gain(ks[5]),
        "w_in_ab": w(ks[6], (n_even, D_MODEL, AB_IN), D_MODEL),
        "w_out_ab": w(ks[7], (n_even, AB_OUT, D_MODEL), AB_OUT),
        "rpb_a": 0.1 * jax.random.normal(ks[8], (n_even, NA_HEADS, 2 * NA_WIN_ROWS - 1, 2 * NA_WIN_COLS - 1), jnp.float32),
        "sink_b": jax.random.normal(ks[9], (n_even, SW_Q_HEADS), jnp.float32),
        "w_in_c": w(ks[10], (n_odd, D_MODEL, C_IN), D_MODEL),
        "w_out_c": w(ks[11], (n_odd, C_OUT, D_MODEL), C_OUT),
        "w_gate": w(ks[12], (DEPTH, D_MODEL, FFN_HIDDEN), D_MODEL),
        "w_up": w(ks[13], (DEPTH, D_MODEL, FFN_HIDDEN), D_MODEL),
        "w_down": w(ks[14], (DEPTH, FFN_HIDDEN, D_MODEL), FFN_HIDDEN),
    }


def reference(x_prompt, x_sample, g_mix_pre, g_mix_post, g_ffn_pre, g_ffn_post, w_in_ab,
              w_out_ab, rpb_a, sink_b, w_in_c, w_out_c, w_gate, w_up, w_down):
    y_prompt = trunk(x_prompt, g_mix_pre, g_mix_post, g_ffn_pre, g_ffn_post, w_in_ab, w_out_ab,
                     rpb_a, sink_b, w_in_c, w_out_c, w_gate, w_up, w_down)
    y_sample = trunk(x_sample, g_mix_pre, g_mix_post, g_ffn_pre, g_ffn_post, w_in_ab, w_out_ab,
                     rpb_a, sink_b, w_in_c, w_out_c, w_gate, w_up, w_down)
    return (y_prompt, y_sample)
```

```python
import math
from contextlib import ExitStack

import numpy as np
import ml_dtypes
import concourse.bass as bass
import concourse.mybir as mybir
from concourse.bass_utils import run_bass_kernel_spmd

F32 = mybir.dt.float32
BF16 = mybir.dt.bfloat16
AF = mybir.ActivationFunctionType
ALU = mybir.AluOpType
NPBF = ml_dtypes.bfloat16

L = 4096
D = 2048
T = 512
NT = L // T
FF = 5632
NFC = FF // 128
NS_RING = 10
ARENA_BYTES = 204 * 1024
EPS = 1e-6


class Buf:
    __slots__ = ("w", "r")

    def __init__(self):
        self.w = {}
        self.r = {}


class _Rec:
    def __getattr__(self, name):
        def f(*a, **k):
            self.call = (name, a, k)
            return self
        return f


class Sched:
    ENG = ("pe", "act", "dve", "pool", "sp")

    def __init__(self, nc):
        self.nc = nc
        self.dry = False
        self.prog = {e: [] for e in self.ENG}
        self.sems = {}
        self.cnt = {}
        self.known = {e: {} for e in self.ENG}
        self.pend = {e: ([], []) for e in self.ENG}
        self._cm = []
        self.ekey = {}
        self.epoch = 0
        for e in self.ENG:
            self._new_sem("E_" + e)
            self.ekey[e] = "E_" + e
        self.pools = {}
        self.n_inst = 0
        self.cuts = []

    def _new_sem(self, name):
        cm = self.nc.semaphore(name)
        h = cm.__enter__()
        self._cm.append(cm)
        self.sems[name] = h
        self.cnt[name] = 0

    def new_pool(self, pname, n):
        names = []
        for i in range(n):
            nm = "D_%s_%d" % (pname, i)
            self._new_sem(nm)
            names.append(nm)
        self.pools[pname] = [names, 0]

    def rotate(self):
        if self.dry:
            return
        self.epoch += 1
        for e in self.ENG:
            nm = "E_%s_%d" % (e, self.epoch)
            self._new_sem(nm)
            self.ekey[e] = nm

    def close(self):
        for cm in reversed(self._cm):
            cm.__exit__(None, None, None)

    def _wait(self, eng, key, val):
        kn = self.known[eng]
        if kn.get(key, 0) >= val:
            return
        kn[key] = val
        sem = self.sems[key]
        self.prog[eng].append(lambda e, sem=sem, val=val: e.wait_ge(sem, val))

    def _deps(self, eng, reads, writes, partial):
        for b in reads:
            for k, v in b.w.items():
                self._wait(eng, k, v)
        for b in writes:
            for k, v in b.r.items():
                self._wait(eng, k, v)
            if not partial:
                for k, v in b.w.items():
                    self._wait(eng, k, v)

    def _commit(self, key, val, reads, writes, partial):
        for b in reads:
            if b.r.get(key, 0) < val:
                b.r[key] = val
        for b in writes:
            if b.r or not partial:
                b.w = {key: val}
                b.r = {}
            else:
                b.w[key] = val

    def op(self, eng, fn, reads=(), writes=(), inc=True, partial=False):
        if self.dry:
            return
        self._deps(eng, reads, writes, partial)
        self.n_inst += 1
        rec = _Rec()
        fn(rec)
        name_, a_, k_ = rec.call

        def fn(e, name_=name_, a_=a_, k_=k_):
            return getattr(e, name_)(*a_, **k_)
        if inc:
            key = self.ekey[eng]
            self.cnt[key] += 1
            val = self.cnt[key]
            sem = self.sems[key]
            self.prog[eng].append(lambda e, fn=fn, sem=sem: fn(e).then_inc(sem, 1))
            pr, pw = self.pend[eng]
            self._commit(key, val, list(reads) + pr, list(writes) + pw, partial)
            self.pend[eng] = ([], [])
        else:
            self.prog[eng].append(lambda e, fn=fn: fn(e))
            self.pend[eng][0].extend(reads)
            self.pend[eng][1].extend(writes)

    def dma(self, q, out, in_, reads=(), writes=(), pool=None, partial=False, sem_name=None):
        if self.dry:
            return
        if sem_name is None:
            names, idx = self.pools[pool or q]
            sem_name = names[idx % len(names)]
            self.pools[pool or q][1] = idx + 1
        if self.cnt[sem_name] > 0:
            self._wait(q, sem_name, self.cnt[sem_name])
        self._deps(q, reads, writes, partial)
        self.cnt[sem_name] += 16
        val = self.cnt[sem_name]
        self.n_inst += 1
        sem = self.sems[sem_name]
        self.prog[q].append(lambda e, out=out, in_=in_, sem=sem: e.dma_start(out=out, in_=in_).then_inc(sem, 16))
        self._commit(sem_name, val, reads, writes, partial)

    def barrier(self, dma_pools=("sp",)):
        if self.dry:
            return
        for e in self.ENG:
            for e2 in self.ENG:
                if e2 != e and self.cnt[self.ekey[e2]] > 0:
                    self._wait(e, self.ekey[e2], self.cnt[self.ekey[e2]])
            for p in dma_pools:
                for nm in self.pools[p][0]:
                    if self.cnt[nm] > 0:
                        self._wait(e, nm, self.cnt[nm])
        self.flush()

    def flush(self):
        prog = self.prog
        if all(len(prog[e]) == 0 for e in self.ENG):
            return

        def run(lst):
            def f(e):
                for fn in lst:
                    fn(e)
            return f
        with self.nc.Block() as block:
            block.tensor(run(prog["pe"]))
            block.scalar(run(prog["act"]))
            block.vector(run(prog["dve"]))
            block.gpsimd(run(prog["pool"]))
            block.sync(run(prog["sp"]))
        self.prog = {e: [] for e in self.ENG}

    def final_wait(self):
        for p, (names, _) in self.pools.items():
            for nm in names:
                if self.cnt[nm] > 0:
                    self._wait("sp", nm, self.cnt[nm])

    def replay(self):
        prog = self.prog
        nc = self.nc

        def run(lst):
            def f(e):
                for fn in lst:
                    fn(e)
            return f
        cuts = list(self.cuts) + [{e: len(prog[e]) for e in self.ENG}]
        prev = {e: 0 for e in self.ENG}
        for cut in cuts:
            if all(cut[e] == prev[e] for e in self.ENG):
                continue
            with nc.Block() as block:
                block.tensor(run(prog["pe"][prev["pe"]:cut["pe"]]))
                block.scalar(run(prog["act"][prev["act"]:cut["act"]]))
                block.vector(run(prog["dve"][prev["dve"]:cut["dve"]]))
                block.gpsimd(run(prog["pool"][prev["pool"]:cut["pool"]]))
                block.sync(run(prog["sp"][prev["sp"]:cut["sp"]]))
            prev = cut


class Arena:
    def __init__(self, nc):
        self.nc = nc
        self.stacks = [ExitStack()]
        self.n = 0

    def mark(self):
        self.stacks.append(ExitStack())
        return len(self.stacks) - 1

    def release(self, m):
        while len(self.stacks) - 1 >= m:
            self.stacks.pop().close()

    def alloc(self, free_shape, dtype):
        self.n += 1
        t = self.stacks[-1].enter_context(self.nc.sbuf_tensor("b%d" % self.n, [128] + list(free_shape), dtype))
        return t[:]

    def psum_banks(self):
        out = []
        for b in range(8):
            self.n += 1
            out.append(self.stacks[-1].enter_context(self.nc.psum_tensor("ps%d" % self.n, [128, 512], F32)))
        return out


def cap(base, delta, dims):
    return bass.AP(base.tensor, base.offset + delta, [list(base.ap[0])] + [list(d) for d in dims])


def dap(handle, offset, dims):
    return bass.AP(handle, offset, [list(d) for d in dims])


def host_consts():
    c = {}
    c["ident"] = np.eye(128, dtype=np.float32).astype(NPBF)
    perm = np.zeros((128, 128), np.float32)
    for f in range(128):
        d = f % 64
        if d < 8:
            perm[f + 8, f] = 1.0
        elif d < 16:
            perm[f - 8, f] = 1.0
    c["perm"] = perm.astype(NPBF)
    inv = np.exp(-math.log(500000.0) * np.arange(8, dtype=np.float32) * (2.0 / 16)).astype(np.float32)
    pos = np.arange(L, dtype=np.float32)
    ang = (pos[:, None] * inv[None, :]).astype(np.float32)
    cosv = np.cos(ang).astype(np.float32)
    sinv = np.sin(ang).astype(np.float32)
    cs = np.zeros((2, 128, L), np.float32)
    cs[0] = 1.0
    for f in range(128):
        d = f % 64
        if d < 8:
            cs[0, f] = cosv[:, d]
            cs[1, f] = -sinv[:, d]
        elif d < 16:
            cs[0, f] = cosv[:, d - 8]
            cs[1, f] = sinv[:, d - 8]
    c["cs"] = cs
    kk = np.arange(128)[:, None]
    qq = np.arange(128)[None, :]
    bands = np.zeros((5, 128, 128), np.float32)
    for i, dlt in enumerate((-1, 0, 1)):
        bands[i] = (np.abs(128 * dlt + kk - qq) <= 64)
    bands[3] = (np.abs(-128 + kk - qq) <= 128)
    bands[4] = (np.abs(128 + kk - qq) <= 128)
    c["bands"] = np.ascontiguousarray(bands.transpose(1, 0, 2)).astype(NPBF)
    am = np.zeros((128, 12, 8, 128), np.float32)
    for rv, R in enumerate((0, 1, 7)):
        p0 = a_p0(R)
        for C in range(4):
            v = rv * 4 + C
            for ch in range(8):
                for rr in range(2):
                    krow = 2 * (p0 + ch) + rr
                    for i in range(8):
                        qrow = 8 * R + i
                        rs = min(max(qrow - 4, 0), 56)
                        if not (rs <= krow < rs + 8):
                            continue
                        for j in range(16):
                            qcol = 16 * C + j
                            c0 = min(max(qcol - 8, 0), 48)
                            am[rr * 64 + c0: rr * 64 + c0 + 16, v, ch, i * 16 + j] = 1.0
    c["amask"] = am.reshape(128, 12 * 8 * 128).astype(NPBF)
    return c


def a_p0(R):
    return 0 if R == 0 else (24 if R == 7 else 4 * R - 2)


def a_rv(R):
    return 0 if R == 0 else (2 if R == 7 else 1)


def build(nseq=2, nlayers=4, dbg=False, phases="E1ABC3"):
    nc = bass.Bass("TRN2", target_bir_lowering=False)
    es = ExitStack()

    def din(name, shape, dt=F32):
        return nc.dram_tensor(name, list(shape), dt, kind="ExternalInput")

    def dint(name, shape, dt=BF16):
        kind = "ExternalOutput" if (dbg and name in ("qk", "vs", "catT", "ocs", "etab")) else "Internal"
        return nc.dram_tensor(name, list(shape), dt, kind=kind)

    xin = din("xin", [nseq * L, D])
    y = nc.dram_tensor("y", [nseq * L, D], F32, kind="ExternalOutput")
    g4 = din("g4", [16, D])
    need3 = "3" in phases
    needc = nlayers > 1

    def dinw(name, shape, needed):
        return din(name, shape if needed else [1, 1, 1])
    w_in_ab = din("w_in_ab", [2, D, 4608])
    w_out_ab = dinw("w_out_ab", [2, D, D], need3)
    rpbp = din("rpbp", [2, 16, 31, 128])
    sink = din("sink", [2, 16])
    w_in_c = dinw("w_in_c", [2, D, 9216], needc)
    w_out_c = dinw("w_out_c", [2, 1024, D], needc and need3)
    w_gate = dinw("w_gate", [4, D, FF], need3)
    w_up = dinw("w_up", [4, D, FF], need3)
    w_down = dinw("w_down", [4, FF, D], need3)
    c_gcol = din("gcolh", [128, 256])
    c_ident = din("ident", [128, 128], BF16)
    c_perm = din("perm", [128, 128], BF16)
    c_cs = din("cs", [2, 128, L])
    c_bands = din("bands", [128, 5, 128], BF16)
    c_amask = din("amask", [128, 12 * 8 * 128], BF16)

    wq_ab = dint("wq_ab", [2, 26, 128, 2048])
    wv_ab = dint("wv_ab", [2, D, 1280])
    wo_ab = dint("wo_ab", [2, D, D])
    wq_c = dint("wq_c", [2, 48, 128, 2048])
    wv_c = dint("wv_c", [2, D, 3072])
    wo_c = dint("wo_c", [2, 1024, D])
    wgu = dint("wgu", [4, 2 * NFC, 128, 2048])
    wd = dint("wd", [4, FF, D])
    qk = dint("qk", [48, 128, L])
    vs = dint("vs", [L, 48 * 65])
    catT = dint("catT", [16, 128, L])
    ocs = dint("ocs", [3, L, 16 * 65], F32)
    etab = dint("etab", [2, 16, 128, 4 * 480])
    ydbg = nc.dram_tensor("ydbg", [L, D], F32, kind="ExternalOutput") if dbg else None

    S = Sched(nc)
    S.new_pool("sp", 24)
    S.new_pool("pool", 24)
    S.new_pool("ring0", NS_RING)

    A = Arena(nc)
    banks = [None] * 8
    banks16 = [None] * 8
    Bbank = [Buf() for _ in range(8)]

    def new_psum():
        ts = A.psum_banks()
        for b in range(8):
            banks[b] = ts[b][:, :]
            banks16[b] = ts[b][:, :].bitcast(BF16)

    ident = A.alloc([128], BF16)
    perm = A.alloc([128], BF16)
    gcol = A.alloc([16, 16], F32)
    ringslots = [A.alloc([2048], BF16) for _ in range(NS_RING)]
    Bring = [Buf() for _ in range(NS_RING)]
    Bconst = Buf()

    def load_consts():
        S.dma("sp", ident, c_ident.ap(), writes=[Bconst], partial=True)
        S.dma("sp", perm, c_perm.ap(), writes=[Bconst], partial=True)
        S.dma("sp", gcol.rearrange("p n d -> p (n d)"), c_gcol.ap(), writes=[Bconst], partial=True)

    Bw = {}

    def cast(dst, src, key):
        b = Bw.setdefault(key, Buf())
        S.dma("pool", dst, src, writes=[b], partial=True)

    def fm_src(handle, base, rowlen, col0, ncols):
        return dap(handle, base + col0, [[rowlen, 128], [128 * rowlen, 16], [1, ncols]])

    def ab_units():
        u = []
        for j in range(8):
            u.append(([(j * 128, 128)], False))
        for j in range(8):
            u.append(([(1024 + j * 128, 128)], False))
        for gp in range(2):
            for j in range(4):
                hA = 4 * (2 * gp) + j
                hB = 4 * (2 * gp + 1) + j
                u.append(([(3072 + hA * 64, 64), (3072 + hB * 64, 64)], True))
        for gp in range(2):
            u.append(([(4096 + gp * 128, 128)], True))
        return u

    def c_units():
        u = []
        for g in range(3):
            for r in range(2):
                for j in range(8):
                    u.append(([(g * 3072 + r * 1024 + j * 128, 128)], True))
        return u

    def emit_casts(layer):
        l2 = layer // 2
        if layer % 2 == 0:
            base = l2 * D * 4608
            for ui, (cols, _) in enumerate(ab_units()):
                dst = wq_ab.ap()[l2, ui].rearrange("p (dc j) -> p dc j", j=128)
                o = 0
                for (c0, ncl) in cols:
                    cast(dst[:, :, o:o + ncl], fm_src(w_in_ab, base, 4608, c0, ncl), ("wq", layer, ui))
                    o += ncl
            for dc in range(16):
                r0 = dc * 128
                cast(wv_ab.ap()[l2, r0:r0 + 128, 0:1024], w_in_ab.ap()[l2, r0:r0 + 128, 2048:3072], ("wv", layer, 0, dc))
                cast(wv_ab.ap()[l2, r0:r0 + 128, 1024:1280], w_in_ab.ap()[l2, r0:r0 + 128, 4352:4608], ("wv", layer, 0, dc))
            for k in range(16 if need3 else 0):
                cast(wo_ab.ap()[l2, k * 128:(k + 1) * 128, :], w_out_ab.ap()[l2, k * 128:(k + 1) * 128, :], ("wo", layer, k))
        else:
            base = l2 * D * 9216
            for ui, (cols, _) in enumerate(c_units()):
                dst = wq_c.ap()[l2, ui].rearrange("p (dc j) -> p dc j", j=128)
                c0, ncl = cols[0]
                cast(dst, fm_src(w_in_c, base, 9216, c0, ncl), ("wq", layer, ui))
            for g in range(3):
                for dc in range(16):
                    r0 = dc * 128
                    cast(wv_c.ap()[l2, r0:r0 + 128, g * 1024:(g + 1) * 1024],
                         w_in_c.ap()[l2, r0:r0 + 128, g * 3072 + 2048:g * 3072 + 3072], ("wv", layer, g, dc))
            for k in range(8 if need3 else 0):
                cast(wo_c.ap()[l2, k * 128:(k + 1) * 128, :], w_out_c.ap()[l2, k * 128:(k + 1) * 128, :], ("wo", layer, k))
        for fc in range(NFC if need3 else 0):
            dstg = wgu.ap()[layer, 2 * fc].rearrange("p (dc j) -> p dc j", j=128)
            dstu = wgu.ap()[layer, 2 * fc + 1].rearrange("p (dc j) -> p dc j", j=128)
            cast(dstg, fm_src(w_gate, layer * D * FF, FF, fc * 128, 128), ("wgu", layer, 2 * fc))
            cast(dstu, fm_src(w_up, layer * D * FF, FF, fc * 128, 128), ("wgu", layer, 2 * fc + 1))
        for k in range(NFC if need3 else 0):
            cast(wd.ap()[layer, k * 128:(k + 1) * 128, :], w_down.ap()[layer, k * 128:(k + 1) * 128, :], ("wd", layer, k))

    class Ring:
        def __init__(self):
            self.plan = []
            self.i = 0
            self.issued = 0
            self.epoch = 0

        def reset(self):
            self.i = 0
            self.issued = 0

        def _issue(self, k):
            src, n, key = self.plan[k]
            s = k % NS_RING
            S.dma("sp", ringslots[s][:, 0:n], src, reads=[Bw[key]], writes=[Bring[s]],
                  sem_name="D_ring%d_%d" % (self.epoch, s))

        def next(self, src, n, key):
            if S.dry:
                self.plan.append((src, n, key))
                return ringslots[0], Bring[0]
            while self.issued < min(len(self.plan), self.i + NS_RING):
                self._issue(self.issued)
                self.issued += 1
            s = self.i % NS_RING
            self.i += 1
            return ringslots[s], Bring[s]

    ring = Ring()

    def w_fm(kind, layer, ui):
        l2 = layer // 2
        if kind == "wq":
            t = wq_ab if layer % 2 == 0 else wq_c
            src = t.ap()[l2, ui]
        else:
            src = wgu.ap()[layer, ui]
        slot, b = ring.next(src, 2048, (kind, layer, ui))
        return slot.rearrange("p (dc j) -> p dc j", j=128), b

    def rstd_from_ss(ss, rs, n, Bss, Brs):
        S.op("dve", lambda e: e.tensor_scalar(out=rs[:, 0:n], in0=ss[:, 0:n], scalar1=EPS, scalar2=None, op0=ALU.add),
             reads=[Bss], writes=[Brs])
        S.op("act", lambda e: e.activation(out=rs[:, 0:n], in_=rs[:, 0:n], func=AF.Sqrt), reads=[Brs], writes=[Brs])
        S.op("dve", lambda e: e.reciprocal(out=rs[:, 0:n], in_=rs[:, 0:n]), reads=[Brs], writes=[Brs])

    ev_ctr = [0]

    def evac_copy(out, in_, reads, writes, partial=False, eng=None):
        if eng is None:
            eng = "act" if ev_ctr[0] % 2 == 0 else "dve"
            ev_ctr[0] += 1
        if eng == "act":
            S.op("act", lambda e: e.activation(out=out, in_=in_, func=AF.Copy), reads=reads, writes=writes, partial=partial)
        else:
            S.op("dve", lambda e: e.tensor_copy(out=out, in_=in_), reads=reads, writes=writes, partial=partial)

    def prenorm_T(xt, Bxt, hn, Bhn, hnT, BhnT, ss, Bss, Brs, nidx, trbanks):
        rs = ss[:, 4:8]
        S.op("dve", lambda e: e.memset(ss[:, 0:4], 0.0), writes=[Bss])
        for st in range(4):
            S.op("act", lambda e, st=st: e.activation(out=hn, in_=xt[:, st, :], func=AF.Square, scale=D ** -0.5,
                                                      accum_out=ss[:, st:st + 1]),
                 reads=[Bxt[st]], writes=[Bhn, Bss], partial=False)
        rstd_from_ss(ss, rs, 4, Bss, Brs)
        k = 0
        for st in range(4):
            S.op("act", lambda e, st=st: e.activation(out=hn, in_=xt[:, st, :], func=AF.Copy, scale=rs[:, st:st + 1]),
                 reads=[Bxt[st], Brs], writes=[Bhn])
            for dcg in range(2):
                bi = trbanks[k % len(trbanks)]
                k += 1
                pv = banks16[bi].rearrange("p (a b) -> p a b", b=128)
                for j in range(8):
                    dc = dcg * 8 + j
                    S.op("pe", lambda e, pv=pv, j=j, dc=dc: e.transpose(out=pv[:, j, :], in_=hn[:, dc * 128:(dc + 1) * 128],
                                                                         identity=ident),
                         reads=[Bhn, Bconst], writes=[Bbank[bi]], inc=(j == 7))
                gsl = gcol[:, nidx, dcg * 8:(dcg + 1) * 8].unsqueeze(2).to_broadcast([128, 8, 128])
                S.op("dve", lambda e, pv=pv, dcg=dcg, st=st, gsl=gsl: e.tensor_tensor(
                    out=hnT[:, dcg * 8:(dcg + 1) * 8, st * 128:(st + 1) * 128], in0=pv, in1=gsl, op=ALU.mult),
                    reads=[Bbank[bi], Bconst], writes=[BhnT], partial=True)

    def lin_tok_pass(actT, BactT, nk, sts, colgroups, wnext, pbanks):
        first = True
        for k in range(nk):
            slot, bsl = wnext(k)
            for si, st in enumerate(sts):
                for ci, (c0, ncl) in enumerate(colgroups):
                    bi = pbanks[si * len(colgroups) + ci]
                    last = (k == nk - 1) or (si == len(sts) - 1 and ci == len(colgroups) - 1)
                    S.op("pe", lambda e, bi=bi, k=k, st=st, c0=c0, ncl=ncl, slot=slot: e.matmul(
                        banks[bi][:, 0:ncl], lhsT=actT[:, k, st * 128:(st + 1) * 128], rhs=slot[:, c0:c0 + ncl],
                        start=(k == 0), stop=(k == nk - 1)),
                        reads=[BactT, bsl], writes=[Bbank[bi]], inc=last, partial=True)
            first = False

    def P1(seq, layer):
        even = layer % 2 == 0
        l2 = layer // 2
        units = ab_units() if even else c_units()
        vsets = [(20, 0)] if even else [(16, 0), (16, 16), (16, 32)]
        m = A.mark()
        new_psum()
        xt = A.alloc([4, D], F32)
        hn = A.alloc([D], BF16)
        hnT = A.alloc([16, T], BF16)
        cst = A.alloc([2, T], F32)
        stage = A.alloc([2, 4, T], BF16)
        nvh = 20 if even else 16
        vst = A.alloc([2, 2, nvh * 65], BF16)
        qbf = A.alloc([T], BF16)
        t1 = A.alloc([T], F32)
        t2 = A.alloc([T], F32)
        ss = A.alloc([8], F32)
        Bxt = [Buf() for _ in range(4)]
        Bhn, BhnT, Bcs, Bss, Brs, Bqbf, Bt1, Bt2 = (Buf() for _ in range(8))
        Bstage = [Buf(), Buf()]
        Bvst = [Buf(), Buf()]
        src = xin if layer == 0 else y
        S.op("dve", lambda e: e.memset(vst, 1.0), writes=Bvst)
        nidx = 0 * 4 + layer
        for tt in range(NT):
            r0 = seq * L + tt * T
            for st in range(4):
                S.dma("sp", xt[:, st, :], src.ap()[r0 + st * 128:r0 + (st + 1) * 128, :], writes=[Bxt[st]])
            if not even or True:
                S.dma("sp", cst, c_cs.ap()[:, :, tt * T:(tt + 1) * T].rearrange("a p t -> p a t"), writes=[Bcs])
            prenorm_T(xt, Bxt, hn, Bhn, hnT, BhnT, ss, Bss, Brs, nidx, [0, 1])
            for ui, (cols, rot) in enumerate(units if "q" not in phases else []):
                if "r" in phases:
                    rot = False
                slot, bsl = w_fm("wq", layer, ui)
                bi = 2 + (ui % 2)
                for dc in range(16):
                    S.op("pe", lambda e, bi=bi, dc=dc, slot=slot: e.matmul(banks[bi], lhsT=slot[:, dc, :], rhs=hnT[:, dc, :],
                                                                           start=(dc == 0), stop=(dc == 15)),
                         reads=[bsl, BhnT], writes=[Bbank[bi]], inc=(dc == 15))
                sb = (ui // 4) % 2
                dst = stage[:, sb, ui % 4, :]
                if not rot:
                    evac_copy(dst, banks[bi], [Bbank[bi]], [Bstage[sb]], partial=True)
                else:
                    pb = 4 + (ui % 2)
                    S.op("act", lambda e, bi=bi: e.activation(out=qbf, in_=banks[bi], func=AF.Copy), reads=[Bbank[bi]], writes=[Bqbf])
                    S.op("pe", lambda e, pb=pb: e.matmul(banks[pb], lhsT=perm, rhs=qbf, start=True, stop=True),
                         reads=[Bqbf, Bconst], writes=[Bbank[pb]])
                    S.op("act", lambda e, bi=bi: e.activation(out=t1, in_=banks[bi], func=AF.Copy), reads=[Bbank[bi]], writes=[Bt1])
                    S.op("act", lambda e, pb=pb: e.activation(out=t2, in_=banks[pb], func=AF.Copy), reads=[Bbank[pb]], writes=[Bt2])
                    S.op("dve", lambda e: e.tensor_tensor(out=t1, in0=t1, in1=cst[:, 0, :], op=ALU.mult),
                         reads=[Bt1, Bcs], writes=[Bt1])
                    S.op("dve", lambda e: e.tensor_tensor(out=t2, in0=t2, in1=cst[:, 1, :], op=ALU.mult),
                         reads=[Bt2, Bcs], writes=[Bt2])
                    S.op("dve", lambda e, dst=dst: e.tensor_tensor(out=dst, in0=t1, in1=t2, op=ALU.add),
                         reads=[Bt1, Bt2], writes=[Bstage[sb]], partial=True)
                if ui % 4 == 3 or ui == len(units) - 1:
                    c0 = ui - (ui % 4)
                    n = ui - c0 + 1
                    S.dma("sp", qk.ap()[c0:c0 + n, :, tt * T:(tt + 1) * T].rearrange("c p t -> p c t"),
                          stage[:, sb, 0:n, :], reads=[Bstage[sb]])
            vb = 0
            for (nh, hoff) in (vsets if "v" not in phases else []):
                ncols = nh * 64
                cgs = []
                c = 0
                while c < ncols:
                    cgs.append((c, min(512, ncols - c)))
                    c += 512
                for sp_ in range(2):
                    sts = [2 * sp_, 2 * sp_ + 1]
                    pb = [2, 3, 4, 5, 6, 7][:2 * len(cgs)]
                    gsel = hoff // 16

                    def wnext(k, gsel=gsel, ncols=ncols):
                        if even:
                            srcw = wv_ab.ap()[l2, k * 128:(k + 1) * 128, :]
                        else:
                            srcw = wv_c.ap()[l2, k * 128:(k + 1) * 128, gsel * 1024:(gsel + 1) * 1024]
                        return ring.next(srcw, ncols, ("wv", layer, gsel, k))
                    lin_tok_pass(hnT, BhnT, 16, sts, cgs, wnext, pb)
                    vbi = vb % 2
                    vb += 1
                    vv = vst[:, vbi, :, :].rearrange("p s (h d) -> p s h d", d=65)
                    for si in range(2):
                        for ci, (c0, ncl) in enumerate(cgs):
                            bi = pb[si * len(cgs) + ci]
                            h0 = c0 // 64
                            nhh = ncl // 64
                            evac_copy(vv[:, si, h0:h0 + nhh, 0:64], banks[bi][:, 0:ncl].rearrange("p (h d) -> p h d", d=64),
                                      [Bbank[bi]], [Bvst[vbi]], partial=True)
                    rr0 = tt * T + sts[0] * 128
                    S.dma("sp", vs.ap()[rr0:rr0 + 256, hoff * 65:(hoff + nh) * 65].rearrange("(s p) c -> p s c", p=128),
                          vst[:, vbi, :, 0:nh * 65], reads=[Bvst[vbi]])
        S.barrier()
        A.release(m)

    def build_etab(l2):
        m = A.mark()
        new_psum()
        zt = A.alloc([2, 480], F32)
        eb = A.alloc([2, 4, 480], BF16)
        Bz = [Buf(), Buf()]
        Be = [Buf(), Buf()]
        k = 0
        for h in range(16):
            for C in range(4):
                zi = k % 2
                k += 1
                base = ((l2 * 16 + h) * 31) * 128 + (48 - 16 * C)
                for rr in range(2):
                    S.dma("sp", zt[rr * 64:(rr + 1) * 64, zi, :].rearrange("p (r j) -> p r j", j=16),
                          dap(rpbp, base + rr * 128, [[1, 64], [128, 30], [1, 16]]), writes=[Bz[zi]], partial=True)
                S.op("act", lambda e, zi=zi, h=h, C=C: e.activation(out=eb[:, h % 2, C, :], in_=zt[:, zi, :], func=AF.Exp),
                     reads=[Bz[zi]], writes=[Be[h % 2]], partial=True)
            S.dma("sp", etab.ap()[l2, h].rearrange("p (c f) -> p c f", f=480), eb[:, h % 2, :, :], reads=[Be[h % 2]])
        S.barrier()
        A.release(m)

    def norm_heads(po_psum, nh, cat_dst, recip, Brec, Bpo, Bcat, posb, Bposb, add_ap=None, extra_reads=()):
        if posb is not None:
            S.op("act", lambda e: e.activation(out=posb[:, 0:nh * 65], in_=po_psum, func=AF.Copy), reads=[Bpo], writes=[Bposb])
            pv = posb[:, 0:nh * 65].rearrange("p (h d) -> p h d", d=65)
            Bsrc = Bposb
        else:
            pv = po_psum.rearrange("p (h d) -> p h d", d=65)
            Bsrc = Bpo
        if add_ap is None:
            S.op("dve", lambda e: e.reciprocal(out=recip[:, 0:nh], in_=pv[:, :, 64]), reads=[Bsrc], writes=[Brec])
        else:
            S.op("dve", lambda e: e.tensor_tensor(out=recip[:, 0:nh], in0=pv[:, :, 64], in1=add_ap, op=ALU.add),
                 reads=[Bsrc] + list(extra_reads), writes=[Brec])
            S.op("dve", lambda e: e.reciprocal(out=recip[:, 0:nh], in_=recip[:, 0:nh]), reads=[Brec], writes=[Brec])
        S.op("dve", lambda e: e.tensor_tensor(out=cat_dst, in0=pv[:, :, 0:64],
                                              in1=recip[:, 0:nh].unsqueeze(2).to_broadcast([128, nh, 64]), op=ALU.mult),
             reads=[Bsrc, Brec], writes=[Bcat])

    def P2A(seq, layer):
        l2 = layer // 2
        m = A.mark()
        new_psum()
        amask = A.alloc([12, 8, 128], BF16)
        etb = A.alloc([2, 2, 4 * 480], BF16)
        qa = A.alloc([2, L], BF16)
        ka = A.alloc([2, L], BF16)
        va = A.alloc([2, 32, 130], BF16)
        pt = A.alloc([2, 8, 128], BF16)
        catb = A.alloc([2, 2, 64], BF16)
        cthp = A.alloc([2, L], BF16)
        recip = A.alloc([8], F32)
        posb = A.alloc([260], F32)
        Bposb = Buf()
        Bam, Brec = Buf(), Buf()
        Bet = [Buf(), Buf()]
        Bq = [Buf(), Buf()]
        Bk = [Buf(), Buf()]
        Bv = [Buf(), Buf()]
        Bpt = [Buf(), Buf()]
        Bcb = [Buf(), Buf()]
        Bct = [Buf(), Buf()]
        S.dma("sp", amask, c_amask.ap().rearrange("p (v c q) -> p v c q", c=8, q=128), writes=[Bam])
        q2 = qa.rearrange("p a t -> p (a t)")
        blk = 0
        hblk = 0
        for hp in range(8):
            hb = hp % 2
            S.dma("sp", qa[:, hb, :], qk.ap()[hp], writes=[Bq[hb]])
            S.dma("sp", ka[:, hb, :], qk.ap()[8 + hp], writes=[Bk[hb]])
            S.dma("sp", va[:, hb, :, :], vs.ap()[:, hp * 130:(hp + 1) * 130].rearrange("(t p) c -> p t c", p=128), writes=[Bv[hb]])
            S.dma("sp", etb[:, hb, :, :], etab.ap()[l2, 2 * hp:2 * hp + 2].rearrange("h p f -> p h f"), writes=[Bet[hb]])
            for R in range(8):
                p0 = a_p0(R)
                rv = a_rv(R)
                rho0 = (8, 4, 0)[rv]
                for C in range(4):
                    v = rv * 4 + C
                    pbi = 4 + (blk % 2)
                    cbi = blk % 2
                    for hh in range(2):
                        ps_ = slice(hh * 64, (hh + 1) * 64)
                        sb0 = 2 * (hblk % 2)
                        pti = hblk % 2
                        hblk += 1
                        qoff = hb * L + (8 * R) * 64 + 16 * C
                        rhs = cap(q2[ps_, :], qoff, [[64, 8], [1, 16]])
                        for c in range(8):
                            bi = sb0 + c // 4
                            S.op("pe", lambda e, bi=bi, c=c, rhs=rhs, p0=p0, ps_=ps_: e.matmul(
                                banks[bi][:, (c % 4) * 128:(c % 4 + 1) * 128],
                                lhsT=ka[ps_, hb, (p0 + c) * 128:(p0 + c + 1) * 128], rhs=rhs, start=True, stop=True),
                                reads=[Bq[hb], Bk[hb]], writes=[Bbank[bi]], inc=(c % 4 == 3), partial=True)
                        for half in range(2):
                            bi = sb0 + half
                            S.op("act", lambda e, bi=bi, half=half, pti=pti: e.activation(
                                out=pt[:, pti, half * 4:(half + 1) * 4, :], in_=banks[bi].rearrange("p (c q) -> p c q", q=128),
                                func=AF.Exp, scale=0.125), reads=[Bbank[bi]], writes=[Bpt[pti]], partial=True)
                        esl = cap(etb[:, hb, hh, :], C * 480 + (rho0 + 7) * 16 + 15, [[32, 8], [-16, 8], [-1, 16]])
                        ptv = pt[:, pti, :, :].rearrange("p c (i j) -> p c i j", j=16)
                        S.op("dve", lambda e, ptv=ptv, esl=esl: e.tensor_tensor(out=ptv, in0=ptv, in1=esl, op=ALU.mult),
                             reads=[Bpt[pti], Bet[hb]], writes=[Bpt[pti]])
                        S.op("dve", lambda e, pti=pti, v=v: e.tensor_tensor(out=pt[:, pti, :, :], in0=pt[:, pti, :, :],
                                                                           in1=amask[:, v, :, :], op=ALU.mult),
                             reads=[Bpt[pti], Bam], writes=[Bpt[pti]])
                        for c in range(8):
                            S.op("pe", lambda e, pbi=pbi, c=c, pti=pti, p0=p0, hh=hh: e.matmul(
                                banks[pbi][:, hh * 65:(hh + 1) * 65], lhsT=pt[:, pti, c, :],
                                rhs=va[:, hb, p0 + c, hh * 65:(hh + 1) * 65], start=(c == 0), stop=(c == 7)),
                                reads=[Bpt[pti], Bv[hb]], writes=[Bbank[pbi]], inc=(c == 7), partial=True)
                    norm_heads(banks[pbi][:, 0:130], 2, catb[:, cbi, :, :], recip, Brec, Bbank[pbi], Bcb[cbi], posb, Bposb)
                    tb = 6 + (blk % 2)
                    S.op("pe", lambda e, tb=tb, cbi=cbi: e.transpose(out=banks16[tb][:, 0:128],
                                                                     in_=catb[:, cbi, :, :].rearrange("p h d -> p (h d)"), identity=ident),
                         reads=[Bcb[cbi], Bconst], writes=[Bbank[tb]])
                    dst = cap(cthp[:, hb, :], (8 * R) * 64 + 16 * C, [[64, 8], [1, 16]])
                    evac_copy(dst, banks16[tb][:, 0:128].rearrange("p (i j) -> p i j", j=16), [Bbank[tb]], [Bct[hb]], partial=True)
                    blk += 1
            S.dma("sp", catT.ap()[hp], cthp[:, hb, :], reads=[Bct[hb]])
        S.barrier()
        A.release(m)

    def P2B(seq, layer):
        l2 = layer // 2
        m = A.mark()
        new_psum()
        qb = A.alloc([4, L], BF16)
        kb = A.alloc([L], BF16)
        vb = A.alloc([32, 130], BF16)
        pt = A.alloc([2, 3, 512], BF16)
        catb = A.alloc([2, 4, 64], BF16)
        ctb = A.alloc([4, L], BF16)
        bm = A.alloc([5, 128], BF16)
        esk = A.alloc([16], F32)
        recip = A.alloc([8], F32)
        posb = A.alloc([260], F32)
        Bposb = Buf()
        Bq, Bk, Bv, Bbm, Bes, Brec, Bct = (Buf() for _ in range(7))
        Bpt = [Buf(), Buf()]
        Bcb = [Buf(), Buf()]
        S.dma("sp", bm, c_bands.ap(), writes=[Bbm])
        S.dma("sp", esk, dap(sink, l2 * 16, [[0, 128], [1, 16]]), writes=[Bes])
        S.op("act", lambda e: e.activation(out=esk, in_=esk, func=AF.Exp), reads=[Bes], writes=[Bes])
        u = 0
        for gp in range(2):
            for j in range(4):
                S.dma("sp", qb[:, j, :], qk.ap()[16 + gp * 4 + j], writes=[Bq], partial=True)
            S.dma("sp", kb, qk.ap()[24 + gp], writes=[Bk])
            S.dma("sp", vb, vs.ap()[:, (16 + 2 * gp) * 65:(16 + 2 * gp + 2) * 65].rearrange("(t p) c -> p t c", p=128), writes=[Bv])
            for gl in range(2):
                g = 2 * gp + gl
                ps_ = slice(gl * 64, (gl + 1) * 64)
                for b in range(32):
                    chunks = [c for c in (b - 1, b, b + 1) if 0 <= c < 32]
                    pti = u % 2
                    sbanks = [0, 1, 2] if u % 2 == 0 else [3, 4, 5]
                    pbi = 6 + (u % 2)
                    u += 1
                    for ci, c in enumerate(chunks):
                        bi = sbanks[ci]
                        S.op("pe", lambda e, bi=bi, c=c, b=b, ps_=ps_: e.matmul(
                            banks[bi].rearrange("p (h q) -> p h q", q=128), lhsT=kb[ps_, c * 128:(c + 1) * 128],
                            rhs=qb[ps_, :, b * 128:(b + 1) * 128], start=True, stop=True),
                            reads=[Bq, Bk], writes=[Bbank[bi]])
                        S.op("act", lambda e, bi=bi, ci=ci, pti=pti: e.activation(out=pt[:, pti, ci, :], in_=banks[bi], func=AF.Exp,
                                                                               scale=0.125),
                             reads=[Bbank[bi]], writes=[Bpt[pti]], partial=True)
                        if c != b:
                            mi = 3 if c < b else 4
                            pv_ = pt[:, pti, ci, :].rearrange("p (h q) -> p h q", q=128)
                            S.op("dve", lambda e, pv_=pv_, mi=mi: e.tensor_tensor(
                                out=pv_, in0=pv_, in1=bm[:, mi, :].unsqueeze(1).to_broadcast([128, 4, 128]), op=ALU.mult),
                                reads=[Bpt[pti], Bbm], writes=[Bpt[pti]])
                    for j in range(4):
                        for ci, c in enumerate(chunks):
                            S.op("pe", lambda e, pbi=pbi, j=j, ci=ci, c=c, pti=pti, gl=gl: e.matmul(
                                banks[pbi][:, j * 65:(j + 1) * 65], lhsT=pt[:, pti, ci, j * 128:(j + 1) * 128],
                                rhs=vb[:, c, gl * 65:(gl + 1) * 65], start=(ci == 0), stop=(ci == len(chunks) - 1)),
                                reads=[Bpt[pti], Bv], writes=[Bbank[pbi]], inc=(ci == len(chunks) - 1 and j == 3), partial=True)
                    cbi = pti
                    norm_heads(banks[pbi][:, 0:260], 4, catb[:, cbi, :, :], recip, Brec, Bbank[pbi], Bcb[cbi], posb, Bposb,
                               add_ap=esk[:, 4 * g:4 * g + 4], extra_reads=[Bes])
                    tb = sbanks[0]
                    tv = banks16[tb][:, 0:256].rearrange("p (k q) -> p k q", q=128)
                    cb2 = catb[:, cbi, :, :].rearrange("p h d -> p (h d)")
                    for k in range(2):
                        S.op("pe", lambda e, tv=tv, k=k, cb2=cb2: e.transpose(out=tv[:, k, :], in_=cb2[:, k * 128:(k + 1) * 128],
                                                                              identity=ident),
                             reads=[Bcb[cbi], Bconst], writes=[Bbank[tb]], inc=(k == 1))
                    evac_copy(ctb[:, gl * 2:gl * 2 + 2, b * 128:(b + 1) * 128], tv, [Bbank[tb]], [Bct], partial=True)
            for k in range(4):
                S.dma("sp", catT.ap()[8 + gp * 4 + k], ctb[:, k, :], reads=[Bct])
        S.barrier()
        A.release(m)

    def P2C(seq, layer):
        m = A.mark()
        new_psum()
        qc = A.alloc([2, L], BF16)
        kc = A.alloc([2, L], BF16)
        vc = A.alloc([2, 32, 130], BF16)
        pt = A.alloc([2, 3, 128], BF16)
        osb = A.alloc([2, 32, 130], F32)
        bm = A.alloc([5, 128], BF16)
        Bbm = Buf()
        Bq = [Buf(), Buf()]
        Bk = [Buf(), Buf()]
        Bv = [Buf(), Buf()]
        Bpt = [Buf(), Buf()]
        Bo = [Buf(), Buf()]
        S.dma("sp", bm, c_bands.ap(), writes=[Bbm])
        it = 0
        u = 0
        for g, d in enumerate((1, 4, 16)):
            nmt = 32 // d
            for hp in range(8):
                hb = it % 2
                it += 1
                S.dma("sp", qc[:, hb, :], qk.ap()[g * 16 + hp], writes=[Bq[hb]])
                S.dma("sp", kc[:, hb, :], qk.ap()[g * 16 + 8 + hp], writes=[Bk[hb]])
                c0 = (g * 16 + 2 * hp) * 65
                for r in range(d):
                    S.dma("sp", vc[:, hb, r * nmt:(r + 1) * nmt, :],
                          dap(vs, r * 3120 + c0, [[d * 3120, 128], [128 * d * 3120, nmt], [1, 130]]), writes=[Bv[hb]], partial=True)
                q2 = qc[:, hb, :]
                k2 = kc[:, hb, :]
                for r in range(d):
                    for mt in range(nmt):
                        ptile = r * nmt + mt
                        chunks = [c for c in (mt - 1, mt, mt + 1) if 0 <= c < nmt]
                        pbi = 6 + (u % 2)
                        u2 = u
                        u += 1
                        for hh in range(2):
                            ps_ = slice(hh * 64, (hh + 1) * 64)
                            bi = (2 * u2 + hh) % 4
                            pti = (2 * u2 + hh) % 2
                            rhs = cap(q2[ps_, :], r + d * mt * 128, [[d, 128]])
                            for ci, c in enumerate(chunks):
                                lhsT = cap(k2[ps_, :], r + d * c * 128, [[d, 128]])
                                S.op("pe", lambda e, bi=bi, ci=ci, lhsT=lhsT, rhs=rhs: e.matmul(
                                    banks[bi][:, ci * 128:(ci + 1) * 128], lhsT=lhsT, rhs=rhs, start=True, stop=True),
                                    reads=[Bq[hb], Bk[hb]], writes=[Bbank[bi]], inc=(ci == len(chunks) - 1), partial=True)
                            n = len(chunks)
                            S.op("act", lambda e, bi=bi, n=n, pti=pti: e.activation(
                                out=pt[:, pti, 0:n, :], in_=banks[bi][:, 0:n * 128].rearrange("p (c q) -> p c q", q=128),
                                func=AF.Exp, scale=0.125), reads=[Bbank[bi]], writes=[Bpt[pti]])
                            m0 = chunks[0] - mt + 1
                            S.op("dve", lambda e, n=n, pti=pti, m0=m0: e.tensor_tensor(
                                out=pt[:, pti, 0:n, :], in0=pt[:, pti, 0:n, :], in1=bm[:, m0:m0 + n, :], op=ALU.mult),
                                reads=[Bpt[pti], Bbm], writes=[Bpt[pti]])
                            for ci, c in enumerate(chunks):
                                S.op("pe", lambda e, pbi=pbi, ci=ci, c=c, pti=pti, hh=hh, r=r: e.matmul(
                                    banks[pbi][:, hh * 65:(hh + 1) * 65], lhsT=pt[:, pti, ci, :],
                                    rhs=vc[:, hb, r * nmt + c, hh * 65:(hh + 1) * 65], start=(ci == 0), stop=(ci == n - 1)),
                                    reads=[Bpt[pti], Bv[hb]], writes=[Bbank[pbi]], inc=(ci == n - 1), partial=True)
                        evac_copy(osb[:, hb, ptile, :], banks[pbi][:, 0:130], [Bbank[pbi]], [Bo[hb]], partial=True, eng="act")
                for r in range(d):
                    S.dma("sp", dap(ocs, g * L * 1040 + r * 1040 + hp * 130, [[d * 1040, 128], [128 * d * 1040, nmt], [1, 130]]),
                          osb[:, hb, r * nmt:(r + 1) * nmt, :], reads=[Bo[hb]])
        S.barrier()
        A.release(m)
        m = A.mark()
        new_psum()
        og = A.alloc([2, 3, 1040], F32)
        osum = A.alloc([1040], F32)
        catc = A.alloc([1024], BF16)
        cstg = A.alloc([2, 8, 512], BF16)
        recip = A.alloc([16], F32)
        Bog = [Buf(), Buf()]
        Bos, Bcat, Brec = Buf(), Buf(), Buf()
        Bcs = [Buf(), Buf()]
        for t in range(32):
            ob = t % 2
            S.dma("sp", og[:, ob, :, :], ocs.ap()[:, t * 128:(t + 1) * 128, :].rearrange("g p c -> p g c"), writes=[Bog[ob]])
            S.op("dve", lambda e, ob=ob: e.tensor_tensor(out=osum, in0=og[:, ob, 0, :], in1=og[:, ob, 1, :], op=ALU.add),
                 reads=[Bog[ob]], writes=[Bos])
            S.op("dve", lambda e, ob=ob: e.tensor_tensor(out=osum, in0=osum, in1=og[:, ob, 2, :], op=ALU.add),
                 reads=[Bog[ob], Bos], writes=[Bos])
            norm_heads(osum, 16, catc.rearrange("p (h d) -> p h d", d=64), recip, Brec, Bos, Bcat, None, None)
            tb = t % 2
            tv = banks16[tb].rearrange("p (k q) -> p k q", q=128)
            for k in range(8):
                S.op("pe", lambda e, tv=tv, k=k: e.transpose(out=tv[:, k, :], in_=catc[:, k * 128:(k + 1) * 128], identity=ident),
                     reads=[Bcat, Bconst], writes=[Bbank[tb]], inc=(k == 7))
            sb = (t // 4) % 2
            evac_copy(cstg[:, sb, :, (t % 4) * 128:(t % 4 + 1) * 128], tv, [Bbank[tb]], [Bcs[sb]], partial=True)
            if t % 4 == 3:
                t0 = (t - 3) * 128
                S.dma("sp", catT.ap()[0:8, :, t0:t0 + 512].rearrange("c p t -> p c t"), cstg[:, sb, :, :], reads=[Bcs[sb]])
        S.barrier()
        A.release(m)

    def P3(seq, layer):
        even = layer % 2 == 0
        l2 = layer // 2
        EC = 16 if even else 8
        m = A.mark()
        new_psum()
        xt = A.alloc([4, D], F32)
        msb = A.alloc([2, D], F32)
        hn = A.alloc([D], BF16)
        hnT = A.alloc([16, T], BF16)
        ctT = A.alloc([EC, T], BF16)
        hid = A.alloc([NFC, T], BF16)
        gbm = A.alloc([D], F32)
        gbf = A.alloc([D], F32)
        sg = A.alloc([2, T], F32)
        ss = A.alloc([8], F32)
        sm = A.alloc([8], F32)
        Bxt = [Buf() for _ in range(4)]
        Bmsb = [Buf(), Buf()]
        Bhn, BhnT, Bct, Bhid, Bg, Bss, Brs, Bsm, Brm = (Buf() for _ in range(9))
        Bsg = [Buf(), Buf()]
        S.dma("sp", gbm, dap(g4, (1 * 4 + layer) * D, [[0, 128], [1, D]]), writes=[Bg], partial=True)
        S.dma("sp", gbf, dap(g4, (3 * 4 + layer) * D, [[0, 128], [1, D]]), writes=[Bg], partial=True)
        src = xin if layer == 0 else y
        cgs4 = [(0, 512), (512, 512), (1024, 512), (1536, 512)]

        def postnorm_residual(sp_, gb):
            rsm = sm[:, 4:8]
            for si in range(2):
                for ci in range(4):
                    bi = si * 4 + ci
                    evac_copy(msb[:, si, ci * 512:(ci + 1) * 512], banks[bi], [Bbank[bi]], [Bmsb[si]], partial=True, eng="act")
            S.op("dve", lambda e: e.memset(sm[:, 0:2], 0.0), writes=[Bsm])
            for si in range(2):
                S.op("act", lambda e, si=si: e.activation(out=hn, in_=msb[:, si, :], func=AF.Square, scale=D ** -0.5,
                                                          accum_out=sm[:, si:si + 1]),
                     reads=[Bmsb[si]], writes=[Bhn, Bsm])
            rstd_from_ss(sm, rsm, 2, Bsm, Brm)
            for si in range(2):
                st = 2 * sp_ + si
                S.op("dve", lambda e, si=si: e.scalar_tensor_tensor(out=msb[:, si, :], in0=msb[:, si, :], scalar=rsm[:, si:si + 1],
                                                                    in1=gb, op0=ALU.mult, op1=ALU.mult),
                     reads=[Bmsb[si], Brm, Bg], writes=[Bmsb[si]])
                S.op("dve", lambda e, si=si, st=st: e.tensor_tensor(out=xt[:, st, :], in0=xt[:, st, :], in1=msb[:, si, :], op=ALU.add),
                     reads=[Bmsb[si], Bxt[st]], writes=[Bxt[st]])

        for tt in range(NT):
            r0 = seq * L + tt * T
            S.dma("sp", ctT, catT.ap()[0:EC, :, tt * T:(tt + 1) * T].rearrange("c p t -> p c t"), writes=[Bct])
            for st in range(4):
                S.dma("sp", xt[:, st, :], src.ap()[r0 + st * 128:r0 + (st + 1) * 128, :], writes=[Bxt[st]])
            for sp_ in range(2):
                def wnext(k):
                    srcw = (wo_ab if even else wo_c).ap()[l2, k * 128:(k + 1) * 128, :]
                    return ring.next(srcw, 2048, ("wo", layer, k))
                lin_tok_pass(ctT, Bct, EC, [2 * sp_, 2 * sp_ + 1], cgs4, wnext, list(range(8)))
                postnorm_residual(sp_, gbm)
            prenorm_T(xt, Bxt, hn, Bhn, hnT, BhnT, ss, Bss, Brs, 2 * 4 + layer, [6, 7])
            for fc in range(NFC):
                bg = (fc % 2) * 2
                bu = bg + 1
                for which, bi in ((0, bg), (1, bu)):
                    slot, bsl = w_fm("wgu", layer, 2 * fc + which)
                    for dc in range(16):
                        S.op("pe", lambda e, bi=bi, dc=dc, slot=slot: e.matmul(banks[bi], lhsT=slot[:, dc, :], rhs=hnT[:, dc, :],
                                                                               start=(dc == 0), stop=(dc == 15)),
                             reads=[bsl, BhnT], writes=[Bbank[bi]], inc=(dc == 15))
                sgi = fc % 2
                S.op("act", lambda e, bg=bg, sgi=sgi: e.activation(out=sg[:, sgi, :], in_=banks[bg], func=AF.Silu),
                     reads=[Bbank[bg]], writes=[Bsg[sgi]])
                S.op("dve", lambda e, bu=bu, sgi=sgi, fc=fc: e.tensor_tensor(out=hid[:, fc, :], in0=banks[bu], in1=sg[:, sgi, :], op=ALU.mult),
                     reads=[Bbank[bu], Bsg[sgi]], writes=[Bhid], partial=True)
            for sp_ in range(2):
                def wnext(k):
                    return ring.next(wd.ap()[layer, k * 128:(k + 1) * 128, :], 2048, ("wd", layer, k))
                lin_tok_pass(hid, Bhid, NFC, [2 * sp_, 2 * sp_ + 1], cgs4, wnext, list(range(8)))
                postnorm_residual(sp_, gbf)
            for st in range(4):
                S.dma("sp", y.ap()[r0 + st * 128:r0 + (st + 1) * 128, :], xt[:, st, :], reads=[Bxt[st]])
        S.barrier()
        A.release(m)

    def program():
        ring.reset()
        for layer in range(nlayers):
            emit_casts(layer)
        S.barrier(dma_pools=("sp", "pool"))
        load_consts()
        if "E" in phases:
            build_etab(0)
            if nlayers > 2:
                build_etab(1)
        for seq in range(nseq):
            for layer in range(nlayers):
                if "z" in phases and layer >= 1:
                    continue
                if "1" in phases:
                    P1(seq, layer)
                if layer % 2 == 0:
                    if "A" in phases:
                        P2A(seq, layer)
                    if "B" in phases:
                        P2B(seq, layer)
                elif "C" in phases:
                    P2C(seq, layer)
                if "3" in phases:
                    P3(seq, layer)
                    if dbg and layer == 0 and seq == 0:
                        for i in range(8):
                            S.dma("sp", ydbg.ap()[i * 512:(i + 1) * 512, :], y.ap()[i * 512:(i + 1) * 512, :])
                        S.barrier()

    S.dry = True
    program()
    S.dry = False
    program()
    S.final_wait()
    S.flush()
    S.close()
    A.release(0)
    es.close()
    return nc, S


_CACHE = {}
NSEQ_PER_LAUNCH = 1


def _get_program():
    if "nc" not in _CACHE:
        _CACHE["nc"], _CACHE["S"] = build(nseq=NSEQ_PER_LAUNCH)
        _CACHE["consts"] = host_consts()
    return _CACHE["nc"], _CACHE["consts"]


def make_inmap(xs, inputs, consts, nc=None):
    g4 = np.concatenate([inputs["g_mix_pre"], inputs["g_mix_post"], inputs["g_ffn_pre"], inputs["g_ffn_post"]], axis=0)
    rpbp = np.zeros((2, 16, 31, 128), np.float32)
    rpbp[:, :, 8:23, 48:79] = inputs["rpb_a"]
    m = {
        "xin": xs, "g4": np.ascontiguousarray(g4, dtype=np.float32),
        "w_in_ab": inputs["w_in_ab"], "w_out_ab": inputs["w_out_ab"], "rpbp": rpbp, "sink": inputs["sink_b"],
        "w_in_c": inputs["w_in_c"], "w_out_c": inputs["w_out_c"], "w_gate": inputs["w_gate"], "w_up": inputs["w_up"],
        "w_down": inputs["w_down"],
    }
    m["gcolh"] = np.ascontiguousarray(m["g4"].reshape(16, 16, 128).transpose(2, 0, 1).reshape(128, 256))
    m.update(consts)
    if nc is not None:
        shapes = {}
        for alloc in nc.allocations:
            try:
                if alloc.kind == "ExternalInput":
                    shapes[alloc.memorylocations[0].name] = tuple(alloc.tensor_shape)
            except Exception:
                pass
        for k in list(m.keys()):
            if k in shapes and tuple(m[k].shape) != shapes[k]:
                m[k] = np.zeros(shapes[k], m[k].dtype)
    return m


def kernel(**inputs):
    inputs = {k: np.asarray(v) for k, v in inputs.items()}
    nc, consts = _get_program()
    xp = inputs["x_prompt"]
    xsm = inputs["x_sample"]
    if NSEQ_PER_LAUNCH == 2:
        in_maps = []
        for c in range(8):
            xs = np.concatenate([xp[c], xsm[c % 2]], axis=0).astype(np.float32, copy=False)
            in_maps.append(make_inmap(xs, inputs, consts))
        res = run_bass_kernel_spmd(nc, in_maps, core_ids=list(range(8)))
        yp = np.stack([res.results[c]["y"][:L] for c in range(8)], axis=0)
        ys = np.stack([res.results[c]["y"][L:] for c in range(2)], axis=0)
        return (yp.astype(np.float32), ys.astype(np.float32))
    in_maps = [make_inmap(np.ascontiguousarray(xp[c], dtype=np.float32), inputs, consts) for c in range(8)]
    res = run_bass_kernel_spmd(nc, in_maps, core_ids=list(range(8)))
    yp = np.stack([res.results[c]["y"] for c in range(8)], axis=0)
    in_maps = [make_inmap(np.ascontiguousarray(xsm[c % 2], dtype=np.float32), inputs, consts) for c in range(8)]
    res = run_bass_kernel_spmd(nc, in_maps, core_ids=list(range(8)))
    ys = np.stack([res.results[c]["y"] for c in range(2)], axis=0)
    return (yp.astype(np.float32), ys.astype(np.float32))
```

```python
import math
from contextlib import ExitStack

import numpy as np
import ml_dtypes
import concourse.bass as bass
import concourse.mybir as mybir
from concourse.bass_utils import run_bass_kernel_spmd

F32 = mybir.dt.float32
BF16 = mybir.dt.bfloat16
AF = mybir.ActivationFunctionType
ALU = mybir.AluOpType
NPBF = ml_dtypes.bfloat16

L = 4096
D = 2048
T = 512
NT = L // T
FF = 5632
NFC = FF // 128
NS_RING = 10
ARENA_BYTES = 204 * 1024
EPS = 1e-6


class Buf:
    __slots__ = ("w", "r")

    def __init__(self):
        self.w = {}
        self.r = {}


class _Rec:
    def __getattr__(self, name):
        def f(*a, **k):
            self.call = (name, a, k)
            return self
        return f


class Sched:
    ENG = ("pe", "act", "dve", "pool", "sp")

    def __init__(self, nc):
        self.nc = nc
        self.dry = False
        self.prog = {e: [] for e in self.ENG}
        self.sems = {}
        self.cnt = {}
        self.known = {e: {} for e in self.ENG}
        self.pend = {e: ([], []) for e in self.ENG}
        self._cm = []
        self.ekey = {}
        self.epoch = 0
        for e in self.ENG:
            self._new_sem("E_" + e)
            self.ekey[e] = "E_" + e
        self.pools = {}
        self.n_inst = 0
        self.cuts = []

    def _new_sem(self, name):
        cm = self.nc.semaphore(name)
        h = cm.__enter__()
        self._cm.append(cm)
        self.sems[name] = h
        self.cnt[name] = 0

    def new_pool(self, pname, n):
        names = []
        for i in range(n):
            nm = "D_%s_%d" % (pname, i)
            self._new_sem(nm)
            names.append(nm)
        self.pools[pname] = [names, 0]

    def rotate(self):
        if self.dry:
            return
        self.epoch += 1
        for e in self.ENG:
            nm = "E_%s_%d" % (e, self.epoch)
            self._new_sem(nm)
            self.ekey[e] = nm

    def close(self):
        for cm in reversed(self._cm):
            cm.__exit__(None, None, None)

    def _wait(self, eng, key, val):
        kn = self.known[eng]
        if kn.get(key, 0) >= val:
            return
        kn[key] = val
        sem = self.sems[key]
        self.prog[eng].append(lambda e, sem=sem, val=val: e.wait_ge(sem, val))

    def _deps(self, eng, reads, writes, partial):
        for b in reads:
            for k, v in b.w.items():
                self._wait(eng, k, v)
        for b in writes:
            for k, v in b.r.items():
                self._wait(eng, k, v)
            if not partial:
                for k, v in b.w.items():
                    self._wait(eng, k, v)

    def _commit(self, key, val, reads, writes, partial):
        for b in reads:
            if b.r.get(key, 0) < val:
                b.r[key] = val
        for b in writes:
            if b.r or not partial:
                b.w = {key: val}
                b.r = {}
            else:
                b.w[key] = val

    def op(self, eng, fn, reads=(), writes=(), inc=True, partial=False):
        if self.dry:
            return
        self._deps(eng, reads, writes, partial)
        self.n_inst += 1
        rec = _Rec()
        fn(rec)
        name_, a_, k_ = rec.call

        def fn(e, name_=name_, a_=a_, k_=k_):
            return getattr(e, name_)(*a_, **k_)
        if inc:
            key = self.ekey[eng]
            self.cnt[key] += 1
            val = self.cnt[key]
            sem = self.sems[key]
            self.prog[eng].append(lambda e, fn=fn, sem=sem: fn(e).then_inc(sem, 1))
            pr, pw = self.pend[eng]
            self._commit(key, val, list(reads) + pr, list(writes) + pw, partial)
            self.pend[eng] = ([], [])
        else:
            self.prog[eng].append(lambda e, fn=fn: fn(e))
            self.pend[eng][0].extend(reads)
            self.pend[eng][1].extend(writes)

    def dma(self, q, out, in_, reads=(), writes=(), pool=None, partial=False, sem_name=None):
        if self.dry:
            return
        if sem_name is None:
            names, idx = self.pools[pool or q]
            sem_name = names[idx % len(names)]
            self.pools[pool or q][1] = idx + 1
        if self.cnt[sem_name] > 0:
            self._wait(q, sem_name, self.cnt[sem_name])
        self._deps(q, reads, writes, partial)
        self.cnt[sem_name] += 16
        val = self.cnt[sem_name]
        self.n_inst += 1
        sem = self.sems[sem_name]
        self.prog[q].append(lambda e, out=out, in_=in_, sem=sem: e.dma_start(out=out, in_=in_).then_inc(sem, 16))
        self._commit(sem_name, val, reads, writes, partial)

    def barrier(self, dma_pools=("sp",)):
        if self.dry:
            return
        for e in self.ENG:
            for e2 in self.ENG:
                if e2 != e and self.cnt[self.ekey[e2]] > 0:
                    self._wait(e, self.ekey[e2], self.cnt[self.ekey[e2]])
            for p in dma_pools:
                for nm in self.pools[p][0]:
                    if self.cnt[nm] > 0:
                        self._wait(e, nm, self.cnt[nm])
        self.flush()

    def flush(self):
        prog = self.prog
        if all(len(prog[e]) == 0 for e in self.ENG):
            return

        def run(lst):
            def f(e):
                for fn in lst:
                    fn(e)
            return f
        with self.nc.Block() as block:
            block.tensor(run(prog["pe"]))
            block.scalar(run(prog["act"]))
            block.vector(run(prog["dve"]))
            block.gpsimd(run(prog["pool"]))
            block.sync(run(prog["sp"]))
        self.prog = {e: [] for e in self.ENG}

    def final_wait(self):
        for p, (names, _) in self.pools.items():
            for nm in names:
                if self.cnt[nm] > 0:
                    self._wait("sp", nm, self.cnt[nm])

    def replay(self):
        prog = self.prog
        nc = self.nc

        def run(lst):
            def f(e):
                for fn in lst:
                    fn(e)
            return f
        cuts = list(self.cuts) + [{e: len(prog[e]) for e in self.ENG}]
        prev = {e: 0 for e in self.ENG}
        for cut in cuts:
            if all(cut[e] == prev[e] for e in self.ENG):
                continue
            with nc.Block() as block:
                block.tensor(run(prog["pe"][prev["pe"]:cut["pe"]]))
                block.scalar(run(prog["act"][prev["act"]:cut["act"]]))
                block.vector(run(prog["dve"][prev["dve"]:cut["dve"]]))
                block.gpsimd(run(prog["pool"][prev["pool"]:cut["pool"]]))
                block.sync(run(prog["sp"][prev["sp"]:cut["sp"]]))
            prev = cut


class Arena:
    def __init__(self, nc):
        self.nc = nc
        self.stacks = [ExitStack()]
        self.n = 0

    def mark(self):
        self.stacks.append(ExitStack())
        return len(self.stacks) - 1

    def release(self, m):
        while len(self.stacks) - 1 >= m:
            self.stacks.pop().close()

    def alloc(self, free_shape, dtype):
        self.n += 1
        t = self.stacks[-1].enter_context(self.nc.sbuf_tensor("b%d" % self.n, [128] + list(free_shape), dtype))
        return t[:]

    def psum_banks(self):
        out = []
        for b in range(8):
            self.n += 1
            out.append(self.stacks[-1].enter_context(self.nc.psum_tensor("ps%d" % self.n, [128, 512], F32)))
        return out


def cap(base, delta, dims):
    return bass.AP(base.tensor, base.offset + delta, [list(base.ap[0])] + [list(d) for d in dims])


def dap(handle, offset, dims):
    return bass.AP(handle, offset, [list(d) for d in dims])


def host_consts():
    c = {}
    c["ident"] = np.eye(128, dtype=np.float32).astype(NPBF)
    perm = np.zeros((128, 128), np.float32)
    for f in range(128):
        d = f % 64
        if d < 8:
            perm[f + 8, f] = 1.0
        elif d < 16:
            perm[f - 8, f] = 1.0
    c["perm"] = perm.astype(NPBF)
    inv = np.exp(-math.log(500000.0) * np.arange(8, dtype=np.float32) * (2.0 / 16)).astype(np.float32)
    pos = np.arange(L, dtype=np.float32)
    ang = (pos[:, None] * inv[None, :]).astype(np.float32)
    cosv = np.cos(ang).astype(np.float32)
    sinv = np.sin(ang).astype(np.float32)
    cs = np.zeros((2, 128, L), np.float32)
    cs[0] = 1.0
    for f in range(128):
        d = f % 64
        if d < 8:
            cs[0, f] = cosv[:, d]
            cs[1, f] = -sinv[:, d]
        elif d < 16:
            cs[0, f] = cosv[:, d - 8]
            cs[1, f] = sinv[:, d - 8]
    c["cs"] = cs
    kk = np.arange(128)[:, None]
    qq = np.arange(128)[None, :]
    bands = np.zeros((5, 128, 128), np.float32)
    for i, dlt in enumerate((-1, 0, 1)):
        bands[i] = (np.abs(128 * dlt + kk - qq) <= 64)
    bands[3] = (np.abs(-128 + kk - qq) <= 128)
    bands[4] = (np.abs(128 + kk - qq) <= 128)
    c["bands"] = np.ascontiguousarray(bands.transpose(1, 0, 2)).astype(NPBF)
    am = np.zeros((128, 12, 8, 128), np.float32)
    for rv, R in enumerate((0, 1, 7)):
        p0 = a_p0(R)
        for C in range(4):
            v = rv * 4 + C
            for ch in range(8):
                for rr in range(2):
                    krow = 2 * (p0 + ch) + rr
                    for i in range(8):
                        qrow = 8 * R + i
                        rs = min(max(qrow - 4, 0), 56)
                        if not (rs <= krow < rs + 8):
                            continue
                        for j in range(16):
                            qcol = 16 * C + j
                            c0 = min(max(qcol - 8, 0), 48)
                            am[rr * 64 + c0: rr * 64 + c0 + 16, v, ch, i * 16 + j] = 1.0
    c["amask"] = am.reshape(128, 12 * 8 * 128).astype(NPBF)
    return c


def a_p0(R):
    return 0 if R == 0 else (24 if R == 7 else 4 * R - 2)


def a_rv(R):
    return 0 if R == 0 else (2 if R == 7 else 1)


def build(nseq=2, nlayers=4, dbg=False, phases="E1ABC3"):
    nc = bass.Bass("TRN2", target_bir_lowering=False)
    es = ExitStack()

    def din(name, shape, dt=F32):
        return nc.dram_tensor(name, list(shape), dt, kind="ExternalInput")

    def dint(name, shape, dt=BF16):
        kind = "ExternalOutput" if (dbg and name in ("qk", "vs", "catT", "ocs", "etab")) else "Internal"
        return nc.dram_tensor(name, list(shape), dt, kind=kind)

    xin = din("xin", [nseq * L, D])
    y = nc.dram_tensor("y", [nseq * L, D], F32, kind="ExternalOutput")
    g4 = din("g4", [16, D])
    need3 = "3" in phases
    needc = nlayers > 1

    def dinw(name, shape, needed):
        return din(name, shape if needed else [1, 1, 1])
    w_in_ab = din("w_in_ab", [2, D, 4608])
    w_out_ab = dinw("w_out_ab", [2, D, D], need3)
    rpbp = din("rpbp", [2, 16, 31, 128])
    sink = din("sink", [2, 16])
    w_in_c = dinw("w_in_c", [2, D, 9216], needc)
    w_out_c = dinw("w_out_c", [2, 1024, D], needc and need3)
    w_gate = dinw("w_gate", [4, D, FF], need3)
    w_up = dinw("w_up", [4, D, FF], need3)
    w_down = dinw("w_down", [4, FF, D], need3)
    c_gcol = din("gcolh", [128, 256])
    c_ident = din("ident", [128, 128], BF16)
    c_perm = din("perm", [128, 128], BF16)
    c_cs = din("cs", [2, 128, L])
    c_bands = din("bands", [128, 5, 128], BF16)
    c_amask = din("amask", [128, 12 * 8 * 128], BF16)

    wq_ab = dint("wq_ab", [2, 26, 128, 2048])
    wv_ab = dint("wv_ab", [2, D, 1280])
    wo_ab = dint("wo_ab", [2, D, D])
    wq_c = dint("wq_c", [2, 48, 128, 2048])
    wv_c = dint("wv_c", [2, D, 3072])
    wo_c = dint("wo_c", [2, 1024, D])
    wgu = dint("wgu", [4, 2 * NFC, 128, 2048])
    wd = dint("wd", [4, FF, D])
    qk = dint("qk", [48, 128, L])
    vs = dint("vs", [L, 48 * 65])
    catT = dint("catT", [16, 128, L])
    ocs = dint("ocs", [3, L, 16 * 65], F32)
    etab = dint("etab", [2, 16, 128, 4 * 480])
    ydbg = nc.dram_tensor("ydbg", [L, D], F32, kind="ExternalOutput") if dbg else None

    S = Sched(nc)
    S.new_pool("sp", 24)
    S.new_pool("pool", 24)
    for ep in range(nseq):
        S.new_pool("ring%d" % ep, NS_RING)

    A = Arena(nc)
    banks = [None] * 8
    banks16 = [None] * 8
    Bbank = [Buf() for _ in range(8)]

    def new_psum():
        ts = A.psum_banks()
        for b in range(8):
            banks[b] = ts[b][:, :]
            banks16[b] = ts[b][:, :].bitcast(BF16)

    ident = A.alloc([128], BF16)
    perm = A.alloc([128], BF16)
    gcol = A.alloc([16, 16], F32)
    ringslots = [A.alloc([2048], BF16) for _ in range(NS_RING)]
    Bring = [Buf() for _ in range(NS_RING)]
    Bconst = Buf()

    def load_consts():
        S.dma("sp", ident, c_ident.ap(), writes=[Bconst], partial=True)
        S.dma("sp", perm, c_perm.ap(), writes=[Bconst], partial=True)
        S.dma("sp", gcol.rearrange("p n d -> p (n d)"), c_gcol.ap(), writes=[Bconst], partial=True)

    Bw = {}

    def cast(dst, src, key):
        b = Bw.setdefault(key, Buf())
        S.dma("pool", dst, src, writes=[b], partial=True)

    def fm_src(handle, base, rowlen, col0, ncols):
        return dap(handle, base + col0, [[rowlen, 128], [128 * rowlen, 16], [1, ncols]])

    def ab_units():
        u = []
        for j in range(8):
            u.append(([(j * 128, 128)], False))
        for j in range(8):
            u.append(([(1024 + j * 128, 128)], False))
        for gp in range(2):
            for j in range(4):
                hA = 4 * (2 * gp) + j
                hB = 4 * (2 * gp + 1) + j
                u.append(([(3072 + hA * 64, 64), (3072 + hB * 64, 64)], True))
        for gp in range(2):
            u.append(([(4096 + gp * 128, 128)], True))
        return u

    def c_units():
        u = []
        for g in range(3):
            for r in range(2):
                for j in range(8):
                    u.append(([(g * 3072 + r * 1024 + j * 128, 128)], True))
        return u

    def emit_casts(layer):
        l2 = layer // 2
        if layer % 2 == 0:
            base = l2 * D * 4608
            for ui, (cols, _) in enumerate(ab_units()):
                dst = wq_ab.ap()[l2, ui].rearrange("p (dc j) -> p dc j", j=128)
                o = 0
                for (c0, ncl) in cols:
                    cast(dst[:, :, o:o + ncl], fm_src(w_in_ab, base, 4608, c0, ncl), ("wq", layer, ui))
                    o += ncl
            for dc in range(16):
                r0 = dc * 128
                cast(wv_ab.ap()[l2, r0:r0 + 128, 0:1024], w_in_ab.ap()[l2, r0:r0 + 128, 2048:3072], ("wv", layer, 0, dc))
                cast(wv_ab.ap()[l2, r0:r0 + 128, 1024:1280], w_in_ab.ap()[l2, r0:r0 + 128, 4352:4608], ("wv", layer, 0, dc))
            for k in range(16 if need3 else 0):
                cast(wo_ab.ap()[l2, k * 128:(k + 1) * 128, :], w_out_ab.ap()[l2, k * 128:(k + 1) * 128, :], ("wo", layer, k))
        else:
            base = l2 * D * 9216
            for ui, (cols, _) in enumerate(c_units()):
                dst = wq_c.ap()[l2, ui].rearrange("p (dc j) -> p dc j", j=128)
                c0, ncl = cols[0]
                cast(dst, fm_src(w_in_c, base, 9216, c0, ncl), ("wq", layer, ui))
            for g in range(3):
                for dc in range(16):
                    r0 = dc * 128
                    cast(wv_c.ap()[l2, r0:r0 + 128, g * 1024:(g + 1) * 1024],
                         w_in_c.ap()[l2, r0:r0 + 128, g * 3072 + 2048:g * 3072 + 3072], ("wv", layer, g, dc))
            for k in range(8 if need3 else 0):
                cast(wo_c.ap()[l2, k * 128:(k + 1) * 128, :], w_out_c.ap()[l2, k * 128:(k + 1) * 128, :], ("wo", layer, k))
        for fc in range(NFC if need3 else 0):
            dstg = wgu.ap()[layer, 2 * fc].rearrange("p (dc j) -> p dc j", j=128)
            dstu = wgu.ap()[layer, 2 * fc + 1].rearrange("p (dc j) -> p dc j", j=128)
            cast(dstg, fm_src(w_gate, layer * D * FF, FF, fc * 128, 128), ("wgu", layer, 2 * fc))
            cast(dstu, fm_src(w_up, layer * D * FF, FF, fc * 128, 128), ("wgu", layer, 2 * fc + 1))
        for k in range(NFC if need3 else 0):
            cast(wd.ap()[layer, k * 128:(k + 1) * 128, :], w_down.ap()[layer, k * 128:(k + 1) * 128, :], ("wd", layer, k))

    class Ring:
        def __init__(self):
            self.plan = []
            self.i = 0
            self.issued = 0
            self.epoch = 0

        def reset(self):
            self.i = 0
            self.issued = 0
            self.epoch = 0

        def _issue(self, k):
            src, n, key = self.plan[k]
            s = k % NS_RING
            S.dma("sp", ringslots[s][:, 0:n], src, reads=[Bw[key]], writes=[Bring[s]],
                  sem_name="D_ring%d_%d" % (self.epoch, s))

        def next(self, src, n, key):
            if S.dry:
                self.plan.append((src, n, key))
                return ringslots[0], Bring[0]
            while self.issued < min(len(self.plan), self.i + NS_RING):
                self._issue(self.issued)
                self.issued += 1
            s = self.i % NS_RING
            self.i += 1
            return ringslots[s], Bring[s]

    ring = Ring()

    def w_fm(kind, layer, ui):
        l2 = layer // 2
        if kind == "wq":
            t = wq_ab if layer % 2 == 0 else wq_c
            src = t.ap()[l2, ui]
        else:
            src = wgu.ap()[layer, ui]
        slot, b = ring.next(src, 2048, (kind, layer, ui))
        return slot.rearrange("p (dc j) -> p dc j", j=128), b

    def rstd_from_ss(ss, rs, n, Bss, Brs):
        S.op("dve", lambda e: e.tensor_scalar(out=rs[:, 0:n], in0=ss[:, 0:n], scalar1=EPS, scalar2=None, op0=ALU.add),
             reads=[Bss], writes=[Brs])
        S.op("act", lambda e: e.activation(out=rs[:, 0:n], in_=rs[:, 0:n], func=AF.Sqrt), reads=[Brs], writes=[Brs])
        S.op("dve", lambda e: e.reciprocal(out=rs[:, 0:n], in_=rs[:, 0:n]), reads=[Brs], writes=[Brs])

    ev_ctr = [0]

    def evac_copy(out, in_, reads, writes, partial=False, eng=None):
        if eng is None:
            eng = "act" if ev_ctr[0] % 2 == 0 else "dve"
            ev_ctr[0] += 1
        if eng == "act":
            S.op("act", lambda e: e.activation(out=out, in_=in_, func=AF.Copy), reads=reads, writes=writes, partial=partial)
        else:
            S.op("dve", lambda e: e.tensor_copy(out=out, in_=in_), reads=reads, writes=writes, partial=partial)

    def prenorm_T(xt, Bxt, hn, Bhn, hnT, BhnT, ss, Bss, Brs, nidx, trbanks):
        rs = ss[:, 4:8]
        S.op("dve", lambda e: e.memset(ss[:, 0:4], 0.0), writes=[Bss])
        for st in range(4):
            S.op("act", lambda e, st=st: e.activation(out=hn, in_=xt[:, st, :], func=AF.Square, scale=D ** -0.5,
                                                      accum_out=ss[:, st:st + 1]),
                 reads=[Bxt[st]], writes=[Bhn, Bss], partial=False)
        rstd_from_ss(ss, rs, 4, Bss, Brs)
        k = 0
        for st in range(4):
            S.op("act", lambda e, st=st: e.activation(out=hn, in_=xt[:, st, :], func=AF.Copy, scale=rs[:, st:st + 1]),
                 reads=[Bxt[st], Brs], writes=[Bhn])
            for dcg in range(2):
                bi = trbanks[k % len(trbanks)]
                k += 1
                pv = banks16[bi].rearrange("p (a b) -> p a b", b=128)
                for j in range(8):
                    dc = dcg * 8 + j
                    S.op("pe", lambda e, pv=pv, j=j, dc=dc: e.transpose(out=pv[:, j, :], in_=hn[:, dc * 128:(dc + 1) * 128],
                                                                         identity=ident),
                         reads=[Bhn, Bconst], writes=[Bbank[bi]], inc=(j == 7))
                gsl = gcol[:, nidx, dcg * 8:(dcg + 1) * 8].unsqueeze(2).to_broadcast([128, 8, 128])
                S.op("dve", lambda e, pv=pv, dcg=dcg, st=st, gsl=gsl: e.tensor_tensor(
                    out=hnT[:, dcg * 8:(dcg + 1) * 8, st * 128:(st + 1) * 128], in0=pv, in1=gsl, op=ALU.mult),
                    reads=[Bbank[bi], Bconst], writes=[BhnT], partial=True)

    def lin_tok_pass(actT, BactT, nk, sts, colgroups, wnext, pbanks):
        first = True
        for k in range(nk):
            slot, bsl = wnext(k)
            for si, st in enumerate(sts):
                for ci, (c0, ncl) in enumerate(colgroups):
                    bi = pbanks[si * len(colgroups) + ci]
                    last = (k == nk - 1) or (si == len(sts) - 1 and ci == len(colgroups) - 1)
                    S.op("pe", lambda e, bi=bi, k=k, st=st, c0=c0, ncl=ncl, slot=slot: e.matmul(
                        banks[bi][:, 0:ncl], lhsT=actT[:, k, st * 128:(st + 1) * 128], rhs=slot[:, c0:c0 + ncl],
                        start=(k == 0), stop=(k == nk - 1)),
                        reads=[BactT, bsl], writes=[Bbank[bi]], inc=last, partial=True)
            first = False

    def P1(seq, layer):
        even = layer % 2 == 0
        l2 = layer // 2
        units = ab_units() if even else c_units()
        vsets = [(20, 0)] if even else [(16, 0), (16, 16), (16, 32)]
        m = A.mark()
        new_psum()
        xt = A.alloc([4, D], F32)
        hn = A.alloc([D], BF16)
        hnT = A.alloc([16, T], BF16)
        cst = A.alloc([2, T], F32)
        stage = A.alloc([2, 4, T], BF16)
        nvh = 20 if even else 16
        vst = A.alloc([2, 2, nvh * 65], BF16)
        qbf = A.alloc([T], BF16)
        t1 = A.alloc([T], F32)
        t2 = A.alloc([T], F32)
        ss = A.alloc([8], F32)
        Bxt = [Buf() for _ in range(4)]
        Bhn, BhnT, Bcs, Bss, Brs, Bqbf, Bt1, Bt2 = (Buf() for _ in range(8))
        Bstage = [Buf(), Buf()]
        Bvst = [Buf(), Buf()]
        src = xin if layer == 0 else y
        S.op("dve", lambda e: e.memset(vst, 1.0), writes=Bvst)
        nidx = 0 * 4 + layer
        for tt in range(NT):
            r0 = seq * L + tt * T
            for st in range(4):
                S.dma("sp", xt[:, st, :], src.ap()[r0 + st * 128:r0 + (st + 1) * 128, :], writes=[Bxt[st]])
            if not even or True:
                S.dma("sp", cst, c_cs.ap()[:, :, tt * T:(tt + 1) * T].rearrange("a p t -> p a t"), writes=[Bcs])
            prenorm_T(xt, Bxt, hn, Bhn, hnT, BhnT, ss, Bss, Brs, nidx, [0, 1])
            for ui, (cols, rot) in enumerate(units if "q" not in phases else []):
                if "r" in phases:
                    rot = False
                slot, bsl = w_fm("wq", layer, ui)
                bi = 2 + (ui % 2)
                for dc in range(16):
                    S.op("pe", lambda e, bi=bi, dc=dc, slot=slot: e.matmul(banks[bi], lhsT=slot[:, dc, :], rhs=hnT[:, dc, :],
                                                                           start=(dc == 0), stop=(dc == 15)),
                         reads=[bsl, BhnT], writes=[Bbank[bi]], inc=(dc == 15))
                sb = (ui // 4) % 2
                dst = stage[:, sb, ui % 4, :]
                if not rot:
                    evac_copy(dst, banks[bi], [Bbank[bi]], [Bstage[sb]], partial=True)
                else:
                    pb = 4 + (ui % 2)
                    S.op("act", lambda e, bi=bi: e.activation(out=qbf, in_=banks[bi], func=AF.Copy), reads=[Bbank[bi]], writes=[Bqbf])
                    S.op("pe", lambda e, pb=pb: e.matmul(banks[pb], lhsT=perm, rhs=qbf, start=True, stop=True),
                         reads=[Bqbf, Bconst], writes=[Bbank[pb]])
                    S.op("act", lambda e, bi=bi: e.activation(out=t1, in_=banks[bi], func=AF.Copy), reads=[Bbank[bi]], writes=[Bt1])
                    S.op("act", lambda e, pb=pb: e.activation(out=t2, in_=banks[pb], func=AF.Copy), reads=[Bbank[pb]], writes=[Bt2])
                    S.op("dve", lambda e: e.tensor_tensor(out=t1, in0=t1, in1=cst[:, 0, :], op=ALU.mult),
                         reads=[Bt1, Bcs], writes=[Bt1])
                    S.op("dve", lambda e: e.tensor_tensor(out=t2, in0=t2, in1=cst[:, 1, :], op=ALU.mult),
                         reads=[Bt2, Bcs], writes=[Bt2])
                    S.op("dve", lambda e, dst=dst: e.tensor_tensor(out=dst, in0=t1, in1=t2, op=ALU.add),
                         reads=[Bt1, Bt2], writes=[Bstage[sb]], partial=True)
                if ui % 4 == 3 or ui == len(units) - 1:
                    c0 = ui - (ui % 4)
                    n = ui - c0 + 1
                    S.dma("sp", qk.ap()[c0:c0 + n, :, tt * T:(tt + 1) * T].rearrange("c p t -> p c t"),
                          stage[:, sb, 0:n, :], reads=[Bstage[sb]])
            vb = 0
            for (nh, hoff) in (vsets if "v" not in phases else []):
                ncols = nh * 64
                cgs = []
                c = 0
                while c < ncols:
                    cgs.append((c, min(512, ncols - c)))
                    c += 512
                for sp_ in range(2):
                    sts = [2 * sp_, 2 * sp_ + 1]
                    pb = [2, 3, 4, 5, 6, 7][:2 * len(cgs)]
                    gsel = hoff // 16

                    def wnext(k, gsel=gsel, ncols=ncols):
                        if even:
                            srcw = wv_ab.ap()[l2, k * 128:(k + 1) * 128, :]
                        else:
                            srcw = wv_c.ap()[l2, k * 128:(k + 1) * 128, gsel * 1024:(gsel + 1) * 1024]
                        return ring.next(srcw, ncols, ("wv", layer, gsel, k))
                    lin_tok_pass(hnT, BhnT, 16, sts, cgs, wnext, pb)
                    vbi = vb % 2
                    vb += 1
                    vv = vst[:, vbi, :, :].rearrange("p s (h d) -> p s h d", d=65)
                    for si in range(2):
                        for ci, (c0, ncl) in enumerate(cgs):
                            bi = pb[si * len(cgs) + ci]
                            h0 = c0 // 64
                            nhh = ncl // 64
                            evac_copy(vv[:, si, h0:h0 + nhh, 0:64], banks[bi][:, 0:ncl].rearrange("p (h d) -> p h d", d=64),
                                      [Bbank[bi]], [Bvst[vbi]], partial=True)
                    rr0 = tt * T + sts[0] * 128
                    S.dma("sp", vs.ap()[rr0:rr0 + 256, hoff * 65:(hoff + nh) * 65].rearrange("(s p) c -> p s c", p=128),
                          vst[:, vbi, :, 0:nh * 65], reads=[Bvst[vbi]])
        S.barrier()
        A.release(m)

    def build_etab(l2):
        m = A.mark()
        new_psum()
        zt = A.alloc([2, 480], F32)
        eb = A.alloc([2, 4, 480], BF16)
        Bz = [Buf(), Buf()]
        Be = [Buf(), Buf()]
        k = 0
        for h in range(16):
            for C in range(4):
                zi = k % 2
                k += 1
                base = ((l2 * 16 + h) * 31) * 128 + (48 - 16 * C)
                for rr in range(2):
                    S.dma("sp", zt[rr * 64:(rr + 1) * 64, zi, :].rearrange("p (r j) -> p r j", j=16),
                          dap(rpbp, base + rr * 128, [[1, 64], [128, 30], [1, 16]]), writes=[Bz[zi]], partial=True)
                S.op("act", lambda e, zi=zi, h=h, C=C: e.activation(out=eb[:, h % 2, C, :], in_=zt[:, zi, :], func=AF.Exp),
                     reads=[Bz[zi]], writes=[Be[h % 2]], partial=True)
            S.dma("sp", etab.ap()[l2, h].rearrange("p (c f) -> p c f", f=480), eb[:, h % 2, :, :], reads=[Be[h % 2]])
        S.barrier()
        A.release(m)

    def norm_heads(po_psum, nh, cat_dst, recip, Brec, Bpo, Bcat, posb, Bposb, add_ap=None, extra_reads=()):
        if posb is not None:
            S.op("act", lambda e: e.activation(out=posb[:, 0:nh * 65], in_=po_psum, func=AF.Copy), reads=[Bpo], writes=[Bposb])
            pv = posb[:, 0:nh * 65].rearrange("p (h d) -> p h d", d=65)
            Bsrc = Bposb
        else:
            pv = po_psum.rearrange("p (h d) -> p h d", d=65)
            Bsrc = Bpo
        if add_ap is None:
            S.op("dve", lambda e: e.reciprocal(out=recip[:, 0:nh], in_=pv[:, :, 64]), reads=[Bsrc], writes=[Brec])
        else:
            S.op("dve", lambda e: e.tensor_tensor(out=recip[:, 0:nh], in0=pv[:, :, 64], in1=add_ap, op=ALU.add),
                 reads=[Bsrc] + list(extra_reads), writes=[Brec])
            S.op("dve", lambda e: e.reciprocal(out=recip[:, 0:nh], in_=recip[:, 0:nh]), reads=[Brec], writes=[Brec])
        S.op("dve", lambda e: e.tensor_tensor(out=cat_dst, in0=pv[:, :, 0:64],
                                              in1=recip[:, 0:nh].unsqueeze(2).to_broadcast([128, nh, 64]), op=ALU.mult),
             reads=[Bsrc, Brec], writes=[Bcat])

    def P2A(seq, layer):
        l2 = layer // 2
        m = A.mark()
        new_psum()
        amask = A.alloc([12, 8, 128], BF16)
        etb = A.alloc([2, 2, 4 * 480], BF16)
        qa = A.alloc([2, L], BF16)
        ka = A.alloc([2, L], BF16)
        va = A.alloc([2, 32, 130], BF16)
        pt = A.alloc([2, 8, 128], BF16)
        catb = A.alloc([2, 2, 64], BF16)
        cthp = A.alloc([2, L], BF16)
        recip = A.alloc([8], F32)
        posb = A.alloc([260], F32)
        Bposb = Buf()
        Bam, Brec = Buf(), Buf()
        Bet = [Buf(), Buf()]
        Bq = [Buf(), Buf()]
        Bk = [Buf(), Buf()]
        Bv = [Buf(), Buf()]
        Bpt = [Buf(), Buf()]
        Bcb = [Buf(), Buf()]
        Bct = [Buf(), Buf()]
        S.dma("sp", amask, c_amask.ap().rearrange("p (v c q) -> p v c q", c=8, q=128), writes=[Bam])
        q2 = qa.rearrange("p a t -> p (a t)")
        blk = 0
        hblk = 0
        for hp in range(8):
            hb = hp % 2
            S.dma("sp", qa[:, hb, :], qk.ap()[hp], writes=[Bq[hb]])
            S.dma("sp", ka[:, hb, :], qk.ap()[8 + hp], writes=[Bk[hb]])
            S.dma("sp", va[:, hb, :, :], vs.ap()[:, hp * 130:(hp + 1) * 130].rearrange("(t p) c -> p t c", p=128), writes=[Bv[hb]])
            S.dma("sp", etb[:, hb, :, :], etab.ap()[l2, 2 * hp:2 * hp + 2].rearrange("h p f -> p h f"), writes=[Bet[hb]])
            for R in range(8):
                p0 = a_p0(R)
                rv = a_rv(R)
                rho0 = (8, 4, 0)[rv]
                for C in range(4):
                    v = rv * 4 + C
                    pbi = 4 + (blk % 2)
                    cbi = blk % 2
                    for hh in range(2):
                        ps_ = slice(hh * 64, (hh + 1) * 64)
                        sb0 = 2 * (hblk % 2)
                        pti = hblk % 2
                        hblk += 1
                        qoff = hb * L + (8 * R) * 64 + 16 * C
                        rhs = cap(q2[ps_, :], qoff, [[64, 8], [1, 16]])
                        for c in range(8):
                            bi = sb0 + c // 4
                            S.op("pe", lambda e, bi=bi, c=c, rhs=rhs, p0=p0, ps_=ps_: e.matmul(
                                banks[bi][:, (c % 4) * 128:(c % 4 + 1) * 128],
                                lhsT=ka[ps_, hb, (p0 + c) * 128:(p0 + c + 1) * 128], rhs=rhs, start=True, stop=True),
                                reads=[Bq[hb], Bk[hb]], writes=[Bbank[bi]], inc=(c % 4 == 3), partial=True)
                        for half in range(2):
                            bi = sb0 + half
                            S.op("act", lambda e, bi=bi, half=half, pti=pti: e.activation(
                                out=pt[:, pti, half * 4:(half + 1) * 4, :], in_=banks[bi].rearrange("p (c q) -> p c q", q=128),
                                func=AF.Exp, scale=0.125), reads=[Bbank[bi]], writes=[Bpt[pti]], partial=True)
                        esl = cap(etb[:, hb, hh, :], C * 480 + (rho0 + 7) * 16 + 15, [[32, 8], [-16, 8], [-1, 16]])
                        ptv = pt[:, pti, :, :].rearrange("p c (i j) -> p c i j", j=16)
                        S.op("dve", lambda e, ptv=ptv, esl=esl: e.tensor_tensor(out=ptv, in0=ptv, in1=esl, op=ALU.mult),
                             reads=[Bpt[pti], Bet[hb]], writes=[Bpt[pti]])
                        S.op("dve", lambda e, pti=pti, v=v: e.tensor_tensor(out=pt[:, pti, :, :], in0=pt[:, pti, :, :],
                                                                           in1=amask[:, v, :, :], op=ALU.mult),
                             reads=[Bpt[pti], Bam], writes=[Bpt[pti]])
                        for c in range(8):
                            S.op("pe", lambda e, pbi=pbi, c=c, pti=pti, p0=p0, hh=hh: e.matmul(
                                banks[pbi][:, hh * 65:(hh + 1) * 65], lhsT=pt[:, pti, c, :],
                                rhs=va[:, hb, p0 + c, hh * 65:(hh + 1) * 65], start=(c == 0), stop=(c == 7)),
                                reads=[Bpt[pti], Bv[hb]], writes=[Bbank[pbi]], inc=(c == 7), partial=True)
                    norm_heads(banks[pbi][:, 0:130], 2, catb[:, cbi, :, :], recip, Brec, Bbank[pbi], Bcb[cbi], posb, Bposb)
                    tb = 6 + (blk % 2)
                    S.op("pe", lambda e, tb=tb, cbi=cbi: e.transpose(out=banks16[tb][:, 0:128],
                                                                     in_=catb[:, cbi, :, :].rearrange("p h d -> p (h d)"), identity=ident),
                         reads=[Bcb[cbi], Bconst], writes=[Bbank[tb]])
                    dst = cap(cthp[:, hb, :], (8 * R) * 64 + 16 * C, [[64, 8], [1, 16]])
                    evac_copy(dst, banks16[tb][:, 0:128].rearrange("p (i j) -> p i j", j=16), [Bbank[tb]], [Bct[hb]], partial=True)
                    blk += 1
            S.dma("sp", catT.ap()[hp], cthp[:, hb, :], reads=[Bct[hb]])
        S.barrier()
        A.release(m)

    def P2B(seq, layer):
        l2 = layer // 2
        m = A.mark()
        new_psum()
        qb = A.alloc([4, L], BF16)
        kb = A.alloc([L], BF16)
        vb = A.alloc([32, 130], BF16)
        pt = A.alloc([2, 3, 512], BF16)
        catb = A.alloc([2, 4, 64], BF16)
        ctb = A.alloc([4, L], BF16)
        bm = A.alloc([5, 128], BF16)
        esk = A.alloc([16], F32)
        recip = A.alloc([8], F32)
        posb = A.alloc([260], F32)
        Bposb = Buf()
        Bq, Bk, Bv, Bbm, Bes, Brec, Bct = (Buf() for _ in range(7))
        Bpt = [Buf(), Buf()]
        Bcb = [Buf(), Buf()]
        S.dma("sp", bm, c_bands.ap(), writes=[Bbm])
        S.dma("sp", esk, dap(sink, l2 * 16, [[0, 128], [1, 16]]), writes=[Bes])
        S.op("act", lambda e: e.activation(out=esk, in_=esk, func=AF.Exp), reads=[Bes], writes=[Bes])
        u = 0
        for gp in range(2):
            for j in range(4):
                S.dma("sp", qb[:, j, :], qk.ap()[16 + gp * 4 + j], writes=[Bq], partial=True)
            S.dma("sp", kb, qk.ap()[24 + gp], writes=[Bk])
            S.dma("sp", vb, vs.ap()[:, (16 + 2 * gp) * 65:(16 + 2 * gp + 2) * 65].rearrange("(t p) c -> p t c", p=128), writes=[Bv])
            for gl in range(2):
                g = 2 * gp + gl
                ps_ = slice(gl * 64, (gl + 1) * 64)
                for b in range(32):
                    chunks = [c for c in (b - 1, b, b + 1) if 0 <= c < 32]
                    pti = u % 2
                    sbanks = [0, 1, 2] if u % 2 == 0 else [3, 4, 5]
                    pbi = 6 + (u % 2)
                    u += 1
                    for ci, c in enumerate(chunks):
                        bi = sbanks[ci]
                        S.op("pe", lambda e, bi=bi, c=c, b=b, ps_=ps_: e.matmul(
                            banks[bi].rearrange("p (h q) -> p h q", q=128), lhsT=kb[ps_, c * 128:(c + 1) * 128],
                            rhs=qb[ps_, :, b * 128:(b + 1) * 128], start=True, stop=True),
                            reads=[Bq, Bk], writes=[Bbank[bi]])
                        S.op("act", lambda e, bi=bi, ci=ci, pti=pti: e.activation(out=pt[:, pti, ci, :], in_=banks[bi], func=AF.Exp,
                                                                               scale=0.125),
                             reads=[Bbank[bi]], writes=[Bpt[pti]], partial=True)
                        if c != b:
                            mi = 3 if c < b else 4
                            pv_ = pt[:, pti, ci, :].rearrange("p (h q) -> p h q", q=128)
                            S.op("dve", lambda e, pv_=pv_, mi=mi: e.tensor_tensor(
                                out=pv_, in0=pv_, in1=bm[:, mi, :].unsqueeze(1).to_broadcast([128, 4, 128]), op=ALU.mult),
                                reads=[Bpt[pti], Bbm], writes=[Bpt[pti]])
                    for j in range(4):
                        for ci, c in enumerate(chunks):
                            S.op("pe", lambda e, pbi=pbi, j=j, ci=ci, c=c, pti=pti, gl=gl: e.matmul(
                                banks[pbi][:, j * 65:(j + 1) * 65], lhsT=pt[:, pti, ci, j * 128:(j + 1) * 128],
                                rhs=vb[:, c, gl * 65:(gl + 1) * 65], start=(ci == 0), stop=(ci == len(chunks) - 1)),
                                reads=[Bpt[pti], Bv], writes=[Bbank[pbi]], inc=(ci == len(chunks) - 1 and j == 3), partial=True)
                    cbi = pti
                    norm_heads(banks[pbi][:, 0:260], 4, catb[:, cbi, :, :], recip, Brec, Bbank[pbi], Bcb[cbi], posb, Bposb,
                               add_ap=esk[:, 4 * g:4 * g + 4], extra_reads=[Bes])
                    tb = sbanks[0]
                    tv = banks16[tb][:, 0:256].rearrange("p (k q) -> p k q", q=128)
                    cb2 = catb[:, cbi, :, :].rearrange("p h d -> p (h d)")
                    for k in range(2):
                        S.op("pe", lambda e, tv=tv, k=k, cb2=cb2: e.transpose(out=tv[:, k, :], in_=cb2[:, k * 128:(k + 1) * 128],
                                                                              identity=ident),
                             reads=[Bcb[cbi], Bconst], writes=[Bbank[tb]], inc=(k == 1))
                    evac_copy(ctb[:, gl * 2:gl * 2 + 2, b * 128:(b + 1) * 128], tv, [Bbank[tb]], [Bct], partial=True)
            for k in range(4):
                S.dma("sp", catT.ap()[8 + gp * 4 + k], ctb[:, k, :], reads=[Bct])
        S.barrier()
        A.release(m)

    def P2C(seq, layer):
        m = A.mark()
        new_psum()
        qc = A.alloc([2, L], BF16)
        kc = A.alloc([2, L], BF16)
        vc = A.alloc([2, 32, 130], BF16)
        pt = A.alloc([2, 3, 128], BF16)
        osb = A.alloc([2, 32, 130], F32)
        bm = A.alloc([5, 128], BF16)
        Bbm = Buf()
        Bq = [Buf(), Buf()]
        Bk = [Buf(), Buf()]
        Bv = [Buf(), Buf()]
        Bpt = [Buf(), Buf()]
        Bo = [Buf(), Buf()]
        S.dma("sp", bm, c_bands.ap(), writes=[Bbm])
        it = 0
        u = 0
        for g, d in enumerate((1, 4, 16)):
            nmt = 32 // d
            for hp in range(8):
                hb = it % 2
                it += 1
                S.dma("sp", qc[:, hb, :], qk.ap()[g * 16 + hp], writes=[Bq[hb]])
                S.dma("sp", kc[:, hb, :], qk.ap()[g * 16 + 8 + hp], writes=[Bk[hb]])
                c0 = (g * 16 + 2 * hp) * 65
                for r in range(d):
                    S.dma("sp", vc[:, hb, r * nmt:(r + 1) * nmt, :],
                          dap(vs, r * 3120 + c0, [[d * 3120, 128], [128 * d * 3120, nmt], [1, 130]]), writes=[Bv[hb]], partial=True)
                q2 = qc[:, hb, :]
                k2 = kc[:, hb, :]
                for r in range(d):
                    for mt in range(nmt):
                        ptile = r * nmt + mt
                        chunks = [c for c in (mt - 1, mt, mt + 1) if 0 <= c < nmt]
                        pbi = 6 + (u % 2)
                        u2 = u
                        u += 1
                        for hh in range(2):
                            ps_ = slice(hh * 64, (hh + 1) * 64)
                            bi = (2 * u2 + hh) % 4
                            pti = (2 * u2 + hh) % 2
                            rhs = cap(q2[ps_, :], r + d * mt * 128, [[d, 128]])
                            for ci, c in enumerate(chunks):
                                lhsT = cap(k2[ps_, :], r + d * c * 128, [[d, 128]])
                                S.op("pe", lambda e, bi=bi, ci=ci, lhsT=lhsT, rhs=rhs: e.matmul(
                                    banks[bi][:, ci * 128:(ci + 1) * 128], lhsT=lhsT, rhs=rhs, start=True, stop=True),
                                    reads=[Bq[hb], Bk[hb]], writes=[Bbank[bi]], inc=(ci == len(chunks) - 1), partial=True)
                            n = len(chunks)
                            S.op("act", lambda e, bi=bi, n=n, pti=pti: e.activation(
                                out=pt[:, pti, 0:n, :], in_=banks[bi][:, 0:n * 128].rearrange("p (c q) -> p c q", q=128),
                                func=AF.Exp, scale=0.125), reads=[Bbank[bi]], writes=[Bpt[pti]])
                            m0 = chunks[0] - mt + 1
                            S.op("dve", lambda e, n=n, pti=pti, m0=m0: e.tensor_tensor(
                                out=pt[:, pti, 0:n, :], in0=pt[:, pti, 0:n, :], in1=bm[:, m0:m0 + n, :], op=ALU.mult),
                                reads=[Bpt[pti], Bbm], writes=[Bpt[pti]])
                            for ci, c in enumerate(chunks):
                                S.op("pe", lambda e, pbi=pbi, ci=ci, c=c, pti=pti, hh=hh, r=r: e.matmul(
                                    banks[pbi][:, hh * 65:(hh + 1) * 65], lhsT=pt[:, pti, ci, :],
                                    rhs=vc[:, hb, r * nmt + c, hh * 65:(hh + 1) * 65], start=(ci == 0), stop=(ci == n - 1)),
                                    reads=[Bpt[pti], Bv[hb]], writes=[Bbank[pbi]], inc=(ci == n - 1), partial=True)
                        evac_copy(osb[:, hb, ptile, :], banks[pbi][:, 0:130], [Bbank[pbi]], [Bo[hb]], partial=True, eng="act")
                for r in range(d):
                    S.dma("sp", dap(ocs, g * L * 1040 + r * 1040 + hp * 130, [[d * 1040, 128], [128 * d * 1040, nmt], [1, 130]]),
                          osb[:, hb, r * nmt:(r + 1) * nmt, :], reads=[Bo[hb]])
        S.barrier()
        A.release(m)
        m = A.mark()
        new_psum()
        og = A.alloc([2, 3, 1040], F32)
        osum = A.alloc([1040], F32)
        catc = A.alloc([1024], BF16)
        cstg = A.alloc([2, 8, 512], BF16)
        recip = A.alloc([16], F32)
        Bog = [Buf(), Buf()]
        Bos, Bcat, Brec = Buf(), Buf(), Buf()
        Bcs = [Buf(), Buf()]
        for t in range(32):
            ob = t % 2
            S.dma("sp", og[:, ob, :, :], ocs.ap()[:, t * 128:(t + 1) * 128, :].rearrange("g p c -> p g c"), writes=[Bog[ob]])
            S.op("dve", lambda e, ob=ob: e.tensor_tensor(out=osum, in0=og[:, ob, 0, :], in1=og[:, ob, 1, :], op=ALU.add),
                 reads=[Bog[ob]], writes=[Bos])
            S.op("dve", lambda e, ob=ob: e.tensor_tensor(out=osum, in0=osum, in1=og[:, ob, 2, :], op=ALU.add),
                 reads=[Bog[ob], Bos], writes=[Bos])
            norm_heads(osum, 16, catc.rearrange("p (h d) -> p h d", d=64), recip, Brec, Bos, Bcat, None, None)
            tb = t % 2
            tv = banks16[tb].rearrange("p (k q) -> p k q", q=128)
            for k in range(8):
                S.op("pe", lambda e, tv=tv, k=k: e.transpose(out=tv[:, k, :], in_=catc[:, k * 128:(k + 1) * 128], identity=ident),
                     reads=[Bcat, Bconst], writes=[Bbank[tb]], inc=(k == 7))
            sb = (t // 4) % 2
            evac_copy(cstg[:, sb, :, (t % 4) * 128:(t % 4 + 1) * 128], tv, [Bbank[tb]], [Bcs[sb]], partial=True)
            if t % 4 == 3:
                t0 = (t - 3) * 128
                S.dma("sp", catT.ap()[0:8, :, t0:t0 + 512].rearrange("c p t -> p c t"), cstg[:, sb, :, :], reads=[Bcs[sb]])
        S.barrier()
        A.release(m)

    def P3(seq, layer):
        even = layer % 2 == 0
        l2 = layer // 2
        EC = 16 if even else 8
        m = A.mark()
        new_psum()
        xt = A.alloc([4, D], F32)
        msb = A.alloc([2, D], F32)
        hn = A.alloc([D], BF16)
        hnT = A.alloc([16, T], BF16)
        ctT = A.alloc([EC, T], BF16)
        hid = A.alloc([NFC, T], BF16)
        gbm = A.alloc([D], F32)
        gbf = A.alloc([D], F32)
        sg = A.alloc([2, T], F32)
        ss = A.alloc([8], F32)
        sm = A.alloc([8], F32)
        Bxt = [Buf() for _ in range(4)]
        Bmsb = [Buf(), Buf()]
        Bhn, BhnT, Bct, Bhid, Bg, Bss, Brs, Bsm, Brm = (Buf() for _ in range(9))
        Bsg = [Buf(), Buf()]
        S.dma("sp", gbm, dap(g4, (1 * 4 + layer) * D, [[0, 128], [1, D]]), writes=[Bg], partial=True)
        S.dma("sp", gbf, dap(g4, (3 * 4 + layer) * D, [[0, 128], [1, D]]), writes=[Bg], partial=True)
        src = xin if layer == 0 else y
        cgs4 = [(0, 512), (512, 512), (1024, 512), (1536, 512)]

        def postnorm_residual(sp_, gb):
            rsm = sm[:, 4:8]
            for si in range(2):
                for ci in range(4):
                    bi = si * 4 + ci
                    evac_copy(msb[:, si, ci * 512:(ci + 1) * 512], banks[bi], [Bbank[bi]], [Bmsb[si]], partial=True, eng="act")
            S.op("dve", lambda e: e.memset(sm[:, 0:2], 0.0), writes=[Bsm])
            for si in range(2):
                S.op("act", lambda e, si=si: e.activation(out=hn, in_=msb[:, si, :], func=AF.Square, scale=D ** -0.5,
                                                          accum_out=sm[:, si:si + 1]),
                     reads=[Bmsb[si]], writes=[Bhn, Bsm])
            rstd_from_ss(sm, rsm, 2, Bsm, Brm)
            for si in range(2):
                st = 2 * sp_ + si
                S.op("dve", lambda e, si=si: e.scalar_tensor_tensor(out=msb[:, si, :], in0=msb[:, si, :], scalar=rsm[:, si:si + 1],
                                                                    in1=gb, op0=ALU.mult, op1=ALU.mult),
                     reads=[Bmsb[si], Brm, Bg], writes=[Bmsb[si]])
                S.op("dve", lambda e, si=si, st=st: e.tensor_tensor(out=xt[:, st, :], in0=xt[:, st, :], in1=msb[:, si, :], op=ALU.add),
                     reads=[Bmsb[si], Bxt[st]], writes=[Bxt[st]])

        for tt in range(NT):
            r0 = seq * L + tt * T
            S.dma("sp", ctT, catT.ap()[0:EC, :, tt * T:(tt + 1) * T].rearrange("c p t -> p c t"), writes=[Bct])
            for st in range(4):
                S.dma("sp", xt[:, st, :], src.ap()[r0 + st * 128:r0 + (st + 1) * 128, :], writes=[Bxt[st]])
            for sp_ in range(2):
                def wnext(k):
                    srcw = (wo_ab if even else wo_c).ap()[l2, k * 128:(k + 1) * 128, :]
                    return ring.next(srcw, 2048, ("wo", layer, k))
                lin_tok_pass(ctT, Bct, EC, [2 * sp_, 2 * sp_ + 1], cgs4, wnext, list(range(8)))
                postnorm_residual(sp_, gbm)
            prenorm_T(xt, Bxt, hn, Bhn, hnT, BhnT, ss, Bss, Brs, 2 * 4 + layer, [6, 7])
            for fc in range(NFC):
                bg = (fc % 2) * 2
                bu = bg + 1
                for which, bi in ((0, bg), (1, bu)):
                    slot, bsl = w_fm("wgu", layer, 2 * fc + which)
                    for dc in range(16):
                        S.op("pe", lambda e, bi=bi, dc=dc, slot=slot: e.matmul(banks[bi], lhsT=slot[:, dc, :], rhs=hnT[:, dc, :],
                                                                               start=(dc == 0), stop=(dc == 15)),
                             reads=[bsl, BhnT], writes=[Bbank[bi]], inc=(dc == 15))
                sgi = fc % 2
                S.op("act", lambda e, bg=bg, sgi=sgi: e.activation(out=sg[:, sgi, :], in_=banks[bg], func=AF.Silu),
                     reads=[Bbank[bg]], writes=[Bsg[sgi]])
                S.op("dve", lambda e, bu=bu, sgi=sgi, fc=fc: e.tensor_tensor(out=hid[:, fc, :], in0=banks[bu], in1=sg[:, sgi, :], op=ALU.mult),
                     reads=[Bbank[bu], Bsg[sgi]], writes=[Bhid], partial=True)
            for sp_ in range(2):
                def wnext(k):
                    return ring.next(wd.ap()[layer, k * 128:(k + 1) * 128, :], 2048, ("wd", layer, k))
                lin_tok_pass(hid, Bhid, NFC, [2 * sp_, 2 * sp_ + 1], cgs4, wnext, list(range(8)))
                postnorm_residual(sp_, gbf)
            for st in range(4):
                S.dma("sp", y.ap()[r0 + st * 128:r0 + (st + 1) * 128, :], xt[:, st, :], reads=[Bxt[st]])
        S.barrier()
        A.release(m)

    def program():
        ring.reset()
        for layer in range(nlayers):
            emit_casts(layer)
        S.barrier(dma_pools=("sp", "pool"))
        load_consts()
        if "E" in phases:
            build_etab(0)
            if nlayers > 2:
                build_etab(1)
        for seq in range(nseq):
            if seq > 0:
                S.rotate()
                ring.epoch = seq
            for layer in range(nlayers):
                if "z" in phases and layer >= 1:
                    continue
                if "1" in phases:
                    P1(seq, layer)
                if layer % 2 == 0:
                    if "A" in phases:
                        P2A(seq, layer)
                    if "B" in phases:
                        P2B(seq, layer)
                elif "C" in phases:
                    P2C(seq, layer)
                if "3" in phases:
                    P3(seq, layer)
                    if dbg and layer == 0 and seq == 0:
                        for i in range(8):
                            S.dma("sp", ydbg.ap()[i * 512:(i + 1) * 512, :], y.ap()[i * 512:(i + 1) * 512, :])
                        S.barrier()

    S.dry = True
    program()
    S.dry = False
    program()
    S.final_wait()
    S.flush()
    S.close()
    A.release(0)
    es.close()
    return nc, S


_CACHE = {}
NSEQ_PER_LAUNCH = 2


def _get_program():
    if "nc" not in _CACHE:
        _CACHE["nc"], _CACHE["S"] = build(nseq=NSEQ_PER_LAUNCH)
        _CACHE["consts"] = host_consts()
    return _CACHE["nc"], _CACHE["consts"]


def make_inmap(xs, inputs, consts, nc=None):
    g4 = np.concatenate([inputs["g_mix_pre"], inputs["g_mix_post"], inputs["g_ffn_pre"], inputs["g_ffn_post"]], axis=0)
    rpbp = np.zeros((2, 16, 31, 128), np.float32)
    rpbp[:, :, 8:23, 48:79] = inputs["rpb_a"]
    m = {
        "xin": xs, "g4": np.ascontiguousarray(g4, dtype=np.float32),
        "w_in_ab": inputs["w_in_ab"], "w_out_ab": inputs["w_out_ab"], "rpbp": rpbp, "sink": inputs["sink_b"],
        "w_in_c": inputs["w_in_c"], "w_out_c": inputs["w_out_c"], "w_gate": inputs["w_gate"], "w_up": inputs["w_up"],
        "w_down": inputs["w_down"],
    }
    m["gcolh"] = np.ascontiguousarray(m["g4"].reshape(16, 16, 128).transpose(2, 0, 1).reshape(128, 256))
    m.update(consts)
    if nc is not None:
        shapes = {}
        for alloc in nc.allocations:
            try:
                if alloc.kind == "ExternalInput":
                    shapes[alloc.memorylocations[0].name] = tuple(alloc.tensor_shape)
            except Exception:
                pass
        for k in list(m.keys()):
            if k in shapes and tuple(m[k].shape) != shapes[k]:
                m[k] = np.zeros(shapes[k], m[k].dtype)
    return m


def kernel(**inputs):
    inputs = {k: np.asarray(v) for k, v in inputs.items()}
    nc, consts = _get_program()
    xp = inputs["x_prompt"]
    xsm = inputs["x_sample"]
    if NSEQ_PER_LAUNCH == 2:
        in_maps = []
        for c in range(8):
            xs = np.concatenate([xp[c], xsm[c % 2]], axis=0).astype(np.float32, copy=False)
            in_maps.append(make_inmap(xs, inputs, consts))
        res = run_bass_kernel_spmd(nc, in_maps, core_ids=list(range(8)))
        yp = np.stack([res.results[c]["y"][:L] for c in range(8)], axis=0)
        ys = np.stack([res.results[c]["y"][L:] for c in range(2)], axis=0)
        return (yp.astype(np.float32), ys.astype(np.float32))
    in_maps = [make_inmap(np.ascontiguousarray(xp[c], dtype=np.float32), inputs, consts) for c in range(8)]
    res = run_bass_kernel_spmd(nc, in_maps, core_ids=list(range(8)))
    yp = np.stack([res.results[c]["y"] for c in range(8)], axis=0)
    in_maps = [make_inmap(np.ascontiguousarray(xsm[c % 2], dtype=np.float32), inputs, consts) for c in range(8)]
    res = run_bass_kernel_spmd(nc, in_maps, core_ids=list(range(8)))
    ys = np.stack([res.results[c]["y"] for c in range(2)], axis=0)
    return (yp.astype(np.float32), ys.astype(np.float32))
```

```python
import math
from contextlib import ExitStack

import numpy as np
import ml_dtypes
import concourse.bass as bass
import concourse.mybir as mybir
from concourse.bass_utils import run_bass_kernel_spmd

F32 = mybir.dt.float32
BF16 = mybir.dt.bfloat16
AF = mybir.ActivationFunctionType
ALU = mybir.AluOpType
NPBF = ml_dtypes.bfloat16

L = 4096
D = 2048
T = 512
NT = L // T
FF = 5632
NFC = FF // 128
NS_RING = 10
ARENA_BYTES = 204 * 1024
EPS = 1e-6


class Buf:
    __slots__ = ("w", "r")

    def __init__(self):
        self.w = {}
        self.r = {}


class _Rec:
    def __getattr__(self, name):
        def f(*a, **k):
            self.call = (name, a, k)
            return self
        return f


class Sched:
    ENG = ("pe", "act", "dve", "pool", "sp")

    def __init__(self, nc):
        self.nc = nc
        self.dry = False
        self.prog = {e: [] for e in self.ENG}
        self.sems = {}
        self.cnt = {}
        self.known = {e: {} for e in self.ENG}
        self.pend = {e: ([], []) for e in self.ENG}
        self._cm = []
        self.ekey = {}
        self.epoch = 0
        for e in self.ENG:
            self._new_sem("E_" + e)
            self.ekey[e] = "E_" + e
        self.pools = {}
        self.n_inst = 0
        self.cuts = []

    def _new_sem(self, name):
        cm = self.nc.semaphore(name)
        h = cm.__enter__()
        self._cm.append(cm)
        self.sems[name] = h
        self.cnt[name] = 0

    def new_pool(self, pname, n):
        names = []
        for i in range(n):
            nm = "D_%s_%d" % (pname, i)
            self._new_sem(nm)
            names.append(nm)
        self.pools[pname] = [names, 0]

    def rotate(self):
        if self.dry:
            return
        self.epoch += 1
        for e in self.ENG:
            nm = "E_%s_%d" % (e, self.epoch)
            self._new_sem(nm)
            self.ekey[e] = nm

    def close(self):
        for cm in reversed(self._cm):
            cm.__exit__(None, None, None)

    def _wait(self, eng, key, val):
        kn = self.known[eng]
        if kn.get(key, 0) >= val:
            return
        kn[key] = val
        sem = self.sems[key]
        self.prog[eng].append(lambda e, sem=sem, val=val: e.wait_ge(sem, val))

    def _deps(self, eng, reads, writes, partial):
        for b in reads:
            for k, v in b.w.items():
                self._wait(eng, k, v)
        for b in writes:
            for k, v in b.r.items():
                self._wait(eng, k, v)
            if not partial:
                for k, v in b.w.items():
                    self._wait(eng, k, v)

    def _commit(self, key, val, reads, writes, partial):
        for b in reads:
            if b.r.get(key, 0) < val:
                b.r[key] = val
        for b in writes:
            if b.r or not partial:
                b.w = {key: val}
                b.r = {}
            else:
                b.w[key] = val

    def op(self, eng, fn, reads=(), writes=(), inc=True, partial=False):
        if self.dry:
            return
        self._deps(eng, reads, writes, partial)
        self.n_inst += 1
        rec = _Rec()
        fn(rec)
        name_, a_, k_ = rec.call

        def fn(e, name_=name_, a_=a_, k_=k_):
            return getattr(e, name_)(*a_, **k_)
        if inc:
            key = self.ekey[eng]
            self.cnt[key] += 1
            val = self.cnt[key]
            sem = self.sems[key]
            self.prog[eng].append(lambda e, fn=fn, sem=sem: fn(e).then_inc(sem, 1))
            pr, pw = self.pend[eng]
            self._commit(key, val, list(reads) + pr, list(writes) + pw, partial)
            self.pend[eng] = ([], [])
        else:
            self.prog[eng].append(lambda e, fn=fn: fn(e))
            self.pend[eng][0].extend(reads)
            self.pend[eng][1].extend(writes)

    def dma(self, q, out, in_, reads=(), writes=(), pool=None, partial=False, sem_name=None):
        if self.dry:
            return
        if sem_name is None:
            names, idx = self.pools[pool or q]
            sem_name = names[idx % len(names)]
            self.pools[pool or q][1] = idx + 1
        if self.cnt[sem_name] > 0:
            self._wait(q, sem_name, self.cnt[sem_name])
        self._deps(q, reads, writes, partial)
        self.cnt[sem_name] += 16
        val = self.cnt[sem_name]
        self.n_inst += 1
        sem = self.sems[sem_name]
        self.prog[q].append(lambda e, out=out, in_=in_, sem=sem: e.dma_start(out=out, in_=in_).then_inc(sem, 16))
        self._commit(sem_name, val, reads, writes, partial)

    def barrier(self, dma_pools=("sp",)):
        if self.dry:
            return
        for e in self.ENG:
            for e2 in self.ENG:
                if e2 != e and self.cnt[self.ekey[e2]] > 0:
                    self._wait(e, self.ekey[e2], self.cnt[self.ekey[e2]])
            for p in dma_pools:
                for nm in self.pools[p][0]:
                    if self.cnt[nm] > 0:
                        self._wait(e, nm, self.cnt[nm])
        self.flush()

    def flush(self):
        prog = self.prog
        if all(len(prog[e]) == 0 for e in self.ENG):
            return

        def run(lst):
            def f(e):
                for fn in lst:
                    fn(e)
            return f
        with self.nc.Block() as block:
            block.tensor(run(prog["pe"]))
            block.scalar(run(prog["act"]))
            block.vector(run(prog["dve"]))
            block.gpsimd(run(prog["pool"]))
            block.sync(run(prog["sp"]))
        self.prog = {e: [] for e in self.ENG}

    def final_wait(self):
        for p, (names, _) in self.pools.items():
            for nm in names:
                if self.cnt[nm] > 0:
                    self._wait("sp", nm, self.cnt[nm])

    def replay(self):
        prog = self.prog
        nc = self.nc

        def run(lst):
            def f(e):
                for fn in lst:
                    fn(e)
            return f
        cuts = list(self.cuts) + [{e: len(prog[e]) for e in self.ENG}]
        prev = {e: 0 for e in self.ENG}
        for cut in cuts:
            if all(cut[e] == prev[e] for e in self.ENG):
                continue
            with nc.Block() as block:
                block.tensor(run(prog["pe"][prev["pe"]:cut["pe"]]))
                block.scalar(run(prog["act"][prev["act"]:cut["act"]]))
                block.vector(run(prog["dve"][prev["dve"]:cut["dve"]]))
                block.gpsimd(run(prog["pool"][prev["pool"]:cut["pool"]]))
                block.sync(run(prog["sp"][prev["sp"]:cut["sp"]]))
            prev = cut


class Arena:
    def __init__(self, nc):
        self.nc = nc
        self.stacks = [ExitStack()]
        self.n = 0

    def mark(self):
        self.stacks.append(ExitStack())
        return len(self.stacks) - 1

    def release(self, m):
        while len(self.stacks) - 1 >= m:
            self.stacks.pop().close()

    def alloc(self, free_shape, dtype):
        self.n += 1
        t = self.stacks[-1].enter_context(self.nc.sbuf_tensor("b%d" % self.n, [128] + list(free_shape), dtype))
        return t[:]

    def psum_banks(self):
        out = []
        for b in range(8):
            self.n += 1
            out.append(self.stacks[-1].enter_context(self.nc.psum_tensor("ps%d" % self.n, [128, 512], F32)))
        return out


def cap(base, delta, dims):
    return bass.AP(base.tensor, base.offset + delta, [list(base.ap[0])] + [list(d) for d in dims])


def dap(handle, offset, dims):
    return bass.AP(handle, offset, [list(d) for d in dims])


def host_consts():
    c = {}
    c["ident"] = np.eye(128, dtype=np.float32).astype(NPBF)
    perm = np.zeros((128, 128), np.float32)
    for f in range(128):
        d = f % 64
        if d < 8:
            perm[f + 8, f] = 1.0
        elif d < 16:
            perm[f - 8, f] = 1.0
    c["perm"] = perm.astype(NPBF)
    inv = np.exp(-math.log(500000.0) * np.arange(8, dtype=np.float32) * (2.0 / 16)).astype(np.float32)
    pos = np.arange(L, dtype=np.float32)
    ang = (pos[:, None] * inv[None, :]).astype(np.float32)
    cosv = np.cos(ang).astype(np.float32)
    sinv = np.sin(ang).astype(np.float32)
    cs = np.zeros((2, 128, L), np.float32)
    cs[0] = 1.0
    for f in range(128):
        d = f % 64
        if d < 8:
            cs[0, f] = cosv[:, d]
            cs[1, f] = -sinv[:, d]
        elif d < 16:
            cs[0, f] = cosv[:, d - 8]
            cs[1, f] = sinv[:, d - 8]
    c["cs"] = cs
    kk = np.arange(128)[:, None]
    qq = np.arange(128)[None, :]
    bands = np.zeros((5, 128, 128), np.float32)
    for i, dlt in enumerate((-1, 0, 1)):
        bands[i] = (np.abs(128 * dlt + kk - qq) <= 64)
    bands[3] = (np.abs(-128 + kk - qq) <= 128)
    bands[4] = (np.abs(128 + kk - qq) <= 128)
    c["bands"] = np.ascontiguousarray(bands.transpose(1, 0, 2)).astype(NPBF)
    am = np.zeros((128, 12, 8, 128), np.float32)
    for rv, R in enumerate((0, 1, 7)):
        p0 = a_p0(R)
        for C in range(4):
            v = rv * 4 + C
            for ch in range(8):
                for rr in range(2):
                    krow = 2 * (p0 + ch) + rr
                    for i in range(8):
                        qrow = 8 * R + i
                        rs = min(max(qrow - 4, 0), 56)
                        if not (rs <= krow < rs + 8):
                            continue
                        for j in range(16):
                            qcol = 16 * C + j
                            c0 = min(max(qcol - 8, 0), 48)
                            am[rr * 64 + c0: rr * 64 + c0 + 16, v, ch, i * 16 + j] = 1.0
    c["amask"] = am.reshape(128, 12 * 8 * 128).astype(NPBF)
    return c


def a_p0(R):
    return 0 if R == 0 else (24 if R == 7 else 4 * R - 2)


def a_rv(R):
    return 0 if R == 0 else (2 if R == 7 else 1)


def build(nseq=2, nlayers=4, dbg=False, phases="E1ABC3"):
    nc = bass.Bass("TRN2", target_bir_lowering=False)
    es = ExitStack()

    def din(name, shape, dt=F32):
        return nc.dram_tensor(name, list(shape), dt, kind="ExternalInput")

    def dint(name, shape, dt=BF16):
        kind = "ExternalOutput" if (dbg and name in ("qk", "vs", "catT", "ocs", "etab")) else "Internal"
        return nc.dram_tensor(name, list(shape), dt, kind=kind)

    xin = din("xin", [nseq * L, D])
    y = nc.dram_tensor("y", [nseq * L, D], F32, kind="ExternalOutput")
    g4 = din("g4", [16, D])
    need3 = "3" in phases
    needc = nlayers > 1

    def dinw(name, shape, needed):
        return din(name, shape if needed else [1, 1, 1])
    w_in_ab = din("w_in_ab", [2, D, 4608])
    w_out_ab = dinw("w_out_ab", [2, D, D], need3)
    rpbp = din("rpbp", [2, 16, 31, 128])
    sink = din("sink", [2, 16])
    w_in_c = dinw("w_in_c", [2, D, 9216], needc)
    w_out_c = dinw("w_out_c", [2, 1024, D], needc and need3)
    w_gate = dinw("w_gate", [4, D, FF], need3)
    w_up = dinw("w_up", [4, D, FF], need3)
    w_down = dinw("w_down", [4, FF, D], need3)
    c_gcol = din("gcolh", [128, 256])
    c_ident = din("ident", [128, 128], BF16)
    c_perm = din("perm", [128, 128], BF16)
    c_cs = din("cs", [2, 128, L])
    c_bands = din("bands", [128, 5, 128], BF16)
    c_amask = din("amask", [128, 12 * 8 * 128], BF16)

    wq_ab = dint("wq_ab", [2, 26, 128, 2048])
    wv_ab = dint("wv_ab", [2, D, 1280])
    wo_ab = dint("wo_ab", [2, D, D])
    wq_c = dint("wq_c", [2, 48, 128, 2048])
    wv_c = dint("wv_c", [2, D, 3072])
    wo_c = dint("wo_c", [2, 1024, D])
    wgu = dint("wgu", [4, 2 * NFC, 128, 2048])
    wd = dint("wd", [4, FF, D])
    qk = dint("qk", [48, 128, L])
    vs = dint("vs", [L, 48 * 65])
    catT = dint("catT", [16, 128, L])
    ocs = dint("ocs", [3, L, 16 * 65], F32)
    etab = dint("etab", [2, 16, 128, 4 * 480])
    ydbg = nc.dram_tensor("ydbg", [L, D], F32, kind="ExternalOutput") if dbg else None

    S = Sched(nc)
    S.new_pool("sp", 24)
    S.new_pool("pool", 24)
    for ep in range(nseq):
        S.new_pool("ring%d" % ep, NS_RING)

    A = Arena(nc)
    banks = [None] * 8
    banks16 = [None] * 8
    Bbank = [Buf() for _ in range(8)]

    def new_psum():
        ts = A.psum_banks()
        for b in range(8):
            banks[b] = ts[b][:, :]
            banks16[b] = ts[b][:, :].bitcast(BF16)

    ident = A.alloc([128], BF16)
    perm = A.alloc([128], BF16)
    gcol = A.alloc([16, 16], F32)
    ringslots = [A.alloc([2048], BF16) for _ in range(NS_RING)]
    Bring = [Buf() for _ in range(NS_RING)]
    Bconst = Buf()

    def load_consts():
        S.dma("sp", ident, c_ident.ap(), writes=[Bconst], partial=True)
        S.dma("sp", perm, c_perm.ap(), writes=[Bconst], partial=True)
        S.dma("sp", gcol.rearrange("p n d -> p (n d)"), c_gcol.ap(), writes=[Bconst], partial=True)

    Bw = {}

    def cast(dst, src, key):
        b = Bw.setdefault(key, Buf())
        S.dma("pool", dst, src, writes=[b], partial=True)

    def fm_src(handle, base, rowlen, col0, ncols):
        return dap(handle, base + col0, [[rowlen, 128], [128 * rowlen, 16], [1, ncols]])

    def ab_units():
        u = []
        for j in range(8):
            u.append(([(j * 128, 128)], False))
        for j in range(8):
            u.append(([(1024 + j * 128, 128)], False))
        for gp in range(2):
            for j in range(4):
                hA = 4 * (2 * gp) + j
                hB = 4 * (2 * gp + 1) + j
                u.append(([(3072 + hA * 64, 64), (3072 + hB * 64, 64)], True))
        for gp in range(2):
            u.append(([(4096 + gp * 128, 128)], True))
        return u

    def c_units():
        u = []
        for g in range(3):
            for r in range(2):
                for j in range(8):
                    u.append(([(g * 3072 + r * 1024 + j * 128, 128)], True))
        return u

    def emit_casts(layer):
        l2 = layer // 2
        if layer % 2 == 0:
            base = l2 * D * 4608
            for ui, (cols, _) in enumerate(ab_units()):
                dst = wq_ab.ap()[l2, ui].rearrange("p (dc j) -> p dc j", j=128)
                o = 0
                for (c0, ncl) in cols:
                    cast(dst[:, :, o:o + ncl], fm_src(w_in_ab, base, 4608, c0, ncl), ("wq", layer, ui))
                    o += ncl
            for dc in range(16):
                r0 = dc * 128
                cast(wv_ab.ap()[l2, r0:r0 + 128, 0:1024], w_in_ab.ap()[l2, r0:r0 + 128, 2048:3072], ("wv", layer, 0, dc))
                cast(wv_ab.ap()[l2, r0:r0 + 128, 1024:1280], w_in_ab.ap()[l2, r0:r0 + 128, 4352:4608], ("wv", layer, 0, dc))
            for k in range(16 if need3 else 0):
                cast(wo_ab.ap()[l2, k * 128:(k + 1) * 128, :], w_out_ab.ap()[l2, k * 128:(k + 1) * 128, :], ("wo", layer, k))
        else:
            base = l2 * D * 9216
            for ui, (cols, _) in enumerate(c_units()):
                dst = wq_c.ap()[l2, ui].rearrange("p (dc j) -> p dc j", j=128)
                c0, ncl = cols[0]
                cast(dst, fm_src(w_in_c, base, 9216, c0, ncl), ("wq", layer, ui))
            for g in range(3):
                for dc in range(16):
                    r0 = dc * 128
                    cast(wv_c.ap()[l2, r0:r0 + 128, g * 1024:(g + 1) * 1024],
                         w_in_c.ap()[l2, r0:r0 + 128, g * 3072 + 2048:g * 3072 + 3072], ("wv", layer, g, dc))
            for k in range(8 if need3 else 0):
                cast(wo_c.ap()[l2, k * 128:(k + 1) * 128, :], w_out_c.ap()[l2, k * 128:(k + 1) * 128, :], ("wo", layer, k))
        for fc in range(NFC if need3 else 0):
            dstg = wgu.ap()[layer, 2 * fc].rearrange("p (dc j) -> p dc j", j=128)
            dstu = wgu.ap()[layer, 2 * fc + 1].rearrange("p (dc j) -> p dc j", j=128)
            cast(dstg, fm_src(w_gate, layer * D * FF, FF, fc * 128, 128), ("wgu", layer, 2 * fc))
            cast(dstu, fm_src(w_up, layer * D * FF, FF, fc * 128, 128), ("wgu", layer, 2 * fc + 1))
        for k in range(NFC if need3 else 0):
            cast(wd.ap()[layer, k * 128:(k + 1) * 128, :], w_down.ap()[layer, k * 128:(k + 1) * 128, :], ("wd", layer, k))

    class Ring:
        def __init__(self):
            self.plan = []
            self.i = 0
            self.issued = 0
            self.epoch = 0

        def reset(self):
            self.i = 0
            self.issued = 0
            self.epoch = 0

        def _issue(self, k):
            src, n, key = self.plan[k]
            s = k % NS_RING
            S.dma("sp", ringslots[s][:, 0:n], src, reads=[Bw[key]], writes=[Bring[s]],
                  sem_name="D_ring%d_%d" % (self.epoch, s))

        def next(self, src, n, key):
            if S.dry:
                self.plan.append((src, n, key))
                return ringslots[0], Bring[0]
            while self.issued < min(len(self.plan), self.i + NS_RING):
                self._issue(self.issued)
                self.issued += 1
            s = self.i % NS_RING
            self.i += 1
            return ringslots[s], Bring[s]

    ring = Ring()

    def w_fm(kind, layer, ui):
        l2 = layer // 2
        if kind == "wq":
            t = wq_ab if layer % 2 == 0 else wq_c
            src = t.ap()[l2, ui]
        else:
            src = wgu.ap()[layer, ui]
        slot, b = ring.next(src, 2048, (kind, layer, ui))
        return slot.rearrange("p (dc j) -> p dc j", j=128), b

    def rstd_from_ss(ss, rs, n, Bss, Brs):
        S.op("dve", lambda e: e.tensor_scalar(out=rs[:, 0:n], in0=ss[:, 0:n], scalar1=EPS, scalar2=None, op0=ALU.add),
             reads=[Bss], writes=[Brs])
        S.op("act", lambda e: e.activation(out=rs[:, 0:n], in_=rs[:, 0:n], func=AF.Sqrt), reads=[Brs], writes=[Brs])
        S.op("dve", lambda e: e.reciprocal(out=rs[:, 0:n], in_=rs[:, 0:n]), reads=[Brs], writes=[Brs])

    ev_ctr = [0]

    def evac_copy(out, in_, reads, writes, partial=False, eng=None):
        if eng is None:
            eng = "act" if ev_ctr[0] % 2 == 0 else "dve"
            ev_ctr[0] += 1
        if eng == "act":
            S.op("act", lambda e: e.activation(out=out, in_=in_, func=AF.Copy), reads=reads, writes=writes, partial=partial)
        else:
            S.op("dve", lambda e: e.tensor_copy(out=out, in_=in_), reads=reads, writes=writes, partial=partial)

    def prenorm_T(xt, Bxt, hn, Bhn, hnT, BhnT, ss, Bss, Brs, nidx, trbanks):
        rs = ss[:, 4:8]
        S.op("dve", lambda e: e.memset(ss[:, 0:4], 0.0), writes=[Bss])
        for st in range(4):
            S.op("act", lambda e, st=st: e.activation(out=hn, in_=xt[:, st, :], func=AF.Square, scale=D ** -0.5,
                                                      accum_out=ss[:, st:st + 1]),
                 reads=[Bxt[st]], writes=[Bhn, Bss], partial=False)
        rstd_from_ss(ss, rs, 4, Bss, Brs)
        k = 0
        for st in range(4):
            S.op("act", lambda e, st=st: e.activation(out=hn, in_=xt[:, st, :], func=AF.Copy, scale=rs[:, st:st + 1]),
                 reads=[Bxt[st], Brs], writes=[Bhn])
            for dcg in range(2):
                bi = trbanks[k % len(trbanks)]
                k += 1
                pv = banks16[bi].rearrange("p (a b) -> p a b", b=128)
                for j in range(8):
                    dc = dcg * 8 + j
                    S.op("pe", lambda e, pv=pv, j=j, dc=dc: e.transpose(out=pv[:, j, :], in_=hn[:, dc * 128:(dc + 1) * 128],
                                                                         identity=ident),
                         reads=[Bhn, Bconst], writes=[Bbank[bi]], inc=(j == 7))
                gsl = gcol[:, nidx, dcg * 8:(dcg + 1) * 8].unsqueeze(2).to_broadcast([128, 8, 128])
                S.op("dve", lambda e, pv=pv, dcg=dcg, st=st, gsl=gsl: e.tensor_tensor(
                    out=hnT[:, dcg * 8:(dcg + 1) * 8, st * 128:(st + 1) * 128], in0=pv, in1=gsl, op=ALU.mult),
                    reads=[Bbank[bi], Bconst], writes=[BhnT], partial=True)

    def lin_tok_pass(actT, BactT, nk, sts, colgroups, wnext, pbanks):
        first = True
        for k in range(nk):
            slot, bsl = wnext(k)
            for si, st in enumerate(sts):
                for ci, (c0, ncl) in enumerate(colgroups):
                    bi = pbanks[si * len(colgroups) + ci]
                    last = (k == nk - 1) or (si == len(sts) - 1 and ci == len(colgroups) - 1)
                    S.op("pe", lambda e, bi=bi, k=k, st=st, c0=c0, ncl=ncl, slot=slot: e.matmul(
                        banks[bi][:, 0:ncl], lhsT=actT[:, k, st * 128:(st + 1) * 128], rhs=slot[:, c0:c0 + ncl],
                        start=(k == 0), stop=(k == nk - 1)),
                        reads=[BactT, bsl], writes=[Bbank[bi]], inc=last, partial=True)
            first = False

    def P1(seq, layer):
        even = layer % 2 == 0
        l2 = layer // 2
        units = ab_units() if even else c_units()
        vsets = [(20, 0)] if even else [(16, 0), (16, 16), (16, 32)]
        m = A.mark()
        new_psum()
        xt2 = A.alloc([2, 4, D], F32)
        hn = A.alloc([D], BF16)
        hnT = A.alloc([16, T], BF16)
        cst = A.alloc([2, T], F32)
        stage = A.alloc([2, 4, T], BF16)
        nvh = 20 if even else 16
        vst = A.alloc([2, 2, nvh * 65], BF16)
        qbf = A.alloc([T], BF16)
        t1 = A.alloc([T], F32)
        t2 = A.alloc([T], F32)
        ss = A.alloc([8], F32)
        Bxt2 = [[Buf() for _ in range(4)] for _ in range(2)]
        Bhn, BhnT, Bcs, Bss, Brs, Bqbf, Bt1, Bt2 = (Buf() for _ in range(8))
        Bstage = [Buf(), Buf()]
        Bvst = [Buf(), Buf()]
        src = xin if layer == 0 else y
        S.op("dve", lambda e: e.memset(vst, 1.0), writes=Bvst)
        nidx = 0 * 4 + layer
        def load_x(tt):
            r0 = seq * L + tt * T
            for st in range(4):
                S.dma("sp", xt2[:, tt % 2, st, :], src.ap()[r0 + st * 128:r0 + (st + 1) * 128, :], writes=[Bxt2[tt % 2][st]])
        load_x(0)
        for tt in range(NT):
            xt = xt2[:, tt % 2, :, :]
            Bxt = Bxt2[tt % 2]
            if tt + 1 < NT:
                load_x(tt + 1)
            if not even or True:
                S.dma("sp", cst, c_cs.ap()[:, :, tt * T:(tt + 1) * T].rearrange("a p t -> p a t"), writes=[Bcs])
            prenorm_T(xt, Bxt, hn, Bhn, hnT, BhnT, ss, Bss, Brs, nidx, [0, 1])
            for ui, (cols, rot) in enumerate(units if "q" not in phases else []):
                if "r" in phases:
                    rot = False
                slot, bsl = w_fm("wq", layer, ui)
                bi = 2 + (ui % 2)
                for dc in range(16):
                    S.op("pe", lambda e, bi=bi, dc=dc, slot=slot: e.matmul(banks[bi], lhsT=slot[:, dc, :], rhs=hnT[:, dc, :],
                                                                           start=(dc == 0), stop=(dc == 15)),
                         reads=[bsl, BhnT], writes=[Bbank[bi]], inc=(dc == 15))
                sb = (ui // 4) % 2
                dst = stage[:, sb, ui % 4, :]
                if not rot:
                    evac_copy(dst, banks[bi], [Bbank[bi]], [Bstage[sb]], partial=True)
                else:
                    pb = 4 + (ui % 2)
                    S.op("act", lambda e, bi=bi: e.activation(out=qbf, in_=banks[bi], func=AF.Copy), reads=[Bbank[bi]], writes=[Bqbf])
                    S.op("pe", lambda e, pb=pb: e.matmul(banks[pb], lhsT=perm, rhs=qbf, start=True, stop=True),
                         reads=[Bqbf, Bconst], writes=[Bbank[pb]])
                    S.op("act", lambda e, bi=bi: e.activation(out=t1, in_=banks[bi], func=AF.Copy), reads=[Bbank[bi]], writes=[Bt1])
                    S.op("act", lambda e, pb=pb: e.activation(out=t2, in_=banks[pb], func=AF.Copy), reads=[Bbank[pb]], writes=[Bt2])
                    S.op("dve", lambda e: e.tensor_tensor(out=t1, in0=t1, in1=cst[:, 0, :], op=ALU.mult),
                         reads=[Bt1, Bcs], writes=[Bt1])
                    S.op("dve", lambda e: e.tensor_tensor(out=t2, in0=t2, in1=cst[:, 1, :], op=ALU.mult),
                         reads=[Bt2, Bcs], writes=[Bt2])
                    S.op("dve", lambda e, dst=dst: e.tensor_tensor(out=dst, in0=t1, in1=t2, op=ALU.add),
                         reads=[Bt1, Bt2], writes=[Bstage[sb]], partial=True)
                if ui % 4 == 3 or ui == len(units) - 1:
                    c0 = ui - (ui % 4)
                    n = ui - c0 + 1
                    S.dma("sp", qk.ap()[c0:c0 + n, :, tt * T:(tt + 1) * T].rearrange("c p t -> p c t"),
                          stage[:, sb, 0:n, :], reads=[Bstage[sb]])
            vb = 0
            for (nh, hoff) in (vsets if "v" not in phases else []):
                ncols = nh * 64
                cgs = []
                c = 0
                while c < ncols:
                    cgs.append((c, min(512, ncols - c)))
                    c += 512
                for sp_ in range(2):
                    sts = [2 * sp_, 2 * sp_ + 1]
                    pb = [2, 3, 4, 5, 6, 7][:2 * len(cgs)]
                    gsel = hoff // 16

                    def wnext(k, gsel=gsel, ncols=ncols):
                        if even:
                            srcw = wv_ab.ap()[l2, k * 128:(k + 1) * 128, :]
                        else:
                            srcw = wv_c.ap()[l2, k * 128:(k + 1) * 128, gsel * 1024:(gsel + 1) * 1024]
                        return ring.next(srcw, ncols, ("wv", layer, gsel, k))
                    lin_tok_pass(hnT, BhnT, 16, sts, cgs, wnext, pb)
                    vbi = vb % 2
                    vb += 1
                    vv = vst[:, vbi, :, :].rearrange("p s (h d) -> p s h d", d=65)
                    for si in range(2):
                        for ci, (c0, ncl) in enumerate(cgs):
                            bi = pb[si * len(cgs) + ci]
                            h0 = c0 // 64
                            nhh = ncl // 64
                            evac_copy(vv[:, si, h0:h0 + nhh, 0:64], banks[bi][:, 0:ncl].rearrange("p (h d) -> p h d", d=64),
                                      [Bbank[bi]], [Bvst[vbi]], partial=True)
                    rr0 = tt * T + sts[0] * 128
                    S.dma("sp", vs.ap()[rr0:rr0 + 256, hoff * 65:(hoff + nh) * 65].rearrange("(s p) c -> p s c", p=128),
                          vst[:, vbi, :, 0:nh * 65], reads=[Bvst[vbi]])
        S.barrier()
        A.release(m)

    def build_etab(l2):
        m = A.mark()
        new_psum()
        zt = A.alloc([2, 480], F32)
        eb = A.alloc([2, 4, 480], BF16)
        Bz = [Buf(), Buf()]
        Be = [Buf(), Buf()]
        k = 0
        for h in range(16):
            for C in range(4):
                zi = k % 2
                k += 1
                base = ((l2 * 16 + h) * 31) * 128 + (48 - 16 * C)
                for rr in range(2):
                    S.dma("sp", zt[rr * 64:(rr + 1) * 64, zi, :].rearrange("p (r j) -> p r j", j=16),
                          dap(rpbp, base + rr * 128, [[1, 64], [128, 30], [1, 16]]), writes=[Bz[zi]], partial=True)
                S.op("act", lambda e, zi=zi, h=h, C=C: e.activation(out=eb[:, h % 2, C, :], in_=zt[:, zi, :], func=AF.Exp),
                     reads=[Bz[zi]], writes=[Be[h % 2]], partial=True)
            S.dma("sp", etab.ap()[l2, h].rearrange("p (c f) -> p c f", f=480), eb[:, h % 2, :, :], reads=[Be[h % 2]])
        S.barrier()
        A.release(m)

    def norm_heads(po_psum, nh, cat_dst, recip, Brec, Bpo, Bcat, posb, Bposb, add_ap=None, extra_reads=()):
        if posb is not None:
            S.op("act", lambda e: e.activation(out=posb[:, 0:nh * 65], in_=po_psum, func=AF.Copy), reads=[Bpo], writes=[Bposb])
            pv = posb[:, 0:nh * 65].rearrange("p (h d) -> p h d", d=65)
            Bsrc = Bposb
        else:
            pv = po_psum.rearrange("p (h d) -> p h d", d=65)
            Bsrc = Bpo
        if add_ap is None:
            S.op("dve", lambda e: e.reciprocal(out=recip[:, 0:nh], in_=pv[:, :, 64]), reads=[Bsrc], writes=[Brec])
        else:
            S.op("dve", lambda e: e.tensor_tensor(out=recip[:, 0:nh], in0=pv[:, :, 64], in1=add_ap, op=ALU.add),
                 reads=[Bsrc] + list(extra_reads), writes=[Brec])
            S.op("dve", lambda e: e.reciprocal(out=recip[:, 0:nh], in_=recip[:, 0:nh]), reads=[Brec], writes=[Brec])
        S.op("dve", lambda e: e.tensor_tensor(out=cat_dst, in0=pv[:, :, 0:64],
                                              in1=recip[:, 0:nh].unsqueeze(2).to_broadcast([128, nh, 64]), op=ALU.mult),
             reads=[Bsrc, Brec], writes=[Bcat])

    def P2A(seq, layer):
        l2 = layer // 2
        m = A.mark()
        new_psum()
        amask = A.alloc([12, 8, 128], BF16)
        etb = A.alloc([2, 2, 4 * 480], BF16)
        qa = A.alloc([2, L], BF16)
        ka = A.alloc([2, L], BF16)
        va = A.alloc([2, 32, 130], BF16)
        pt = A.alloc([2, 8, 128], BF16)
        catb = A.alloc([2, 2, 64], BF16)
        cthp = A.alloc([2, L], BF16)
        recip = A.alloc([8], F32)
        posb = A.alloc([260], F32)
        Bposb = Buf()
        Bam, Brec = Buf(), Buf()
        Bet = [Buf(), Buf()]
        Bq = [Buf(), Buf()]
        Bk = [Buf(), Buf()]
        Bv = [Buf(), Buf()]
        Bpt = [Buf(), Buf()]
        Bcb = [Buf(), Buf()]
        Bct = [Buf(), Buf()]
        S.dma("sp", amask, c_amask.ap().rearrange("p (v c q) -> p v c q", c=8, q=128), writes=[Bam])
        q2 = qa.rearrange("p a t -> p (a t)")
        blk = 0
        hblk = 0
        for hp in range(8):
            hb = hp % 2
            S.dma("sp", qa[:, hb, :], qk.ap()[hp], writes=[Bq[hb]])
            S.dma("sp", ka[:, hb, :], qk.ap()[8 + hp], writes=[Bk[hb]])
            S.dma("sp", va[:, hb, :, :], vs.ap()[:, hp * 130:(hp + 1) * 130].rearrange("(t p) c -> p t c", p=128), writes=[Bv[hb]])
            S.dma("sp", etb[:, hb, :, :], etab.ap()[l2, 2 * hp:2 * hp + 2].rearrange("h p f -> p h f"), writes=[Bet[hb]])
            for R in range(8):
                p0 = a_p0(R)
                rv = a_rv(R)
                rho0 = (8, 4, 0)[rv]
                for C in range(4):
                    v = rv * 4 + C
                    pbi = 4 + (blk % 2)
                    cbi = blk % 2
                    for hh in range(2):
                        ps_ = slice(hh * 64, (hh + 1) * 64)
                        sb0 = 2 * (hblk % 2)
                        pti = hblk % 2
                        hblk += 1
                        qoff = hb * L + (8 * R) * 64 + 16 * C
                        rhs = cap(q2[ps_, :], qoff, [[64, 8], [1, 16]])
                        for c in range(8):
                            bi = sb0 + c // 4
                            S.op("pe", lambda e, bi=bi, c=c, rhs=rhs, p0=p0, ps_=ps_: e.matmul(
                                banks[bi][:, (c % 4) * 128:(c % 4 + 1) * 128],
                                lhsT=ka[ps_, hb, (p0 + c) * 128:(p0 + c + 1) * 128], rhs=rhs, start=True, stop=True),
                                reads=[Bq[hb], Bk[hb]], writes=[Bbank[bi]], inc=(c % 4 == 3), partial=True)
                        for half in range(2):
                            bi = sb0 + half
                            S.op("act", lambda e, bi=bi, half=half, pti=pti: e.activation(
                                out=pt[:, pti, half * 4:(half + 1) * 4, :], in_=banks[bi].rearrange("p (c q) -> p c q", q=128),
                                func=AF.Exp, scale=0.125), reads=[Bbank[bi]], writes=[Bpt[pti]], partial=True)
                        esl = cap(etb[:, hb, hh, :], C * 480 + (rho0 + 7) * 16 + 15, [[32, 8], [-16, 8], [-1, 16]])
                        ptv = pt[:, pti, :, :].rearrange("p c (i j) -> p c i j", j=16)
                        S.op("dve", lambda e, ptv=ptv, esl=esl: e.tensor_tensor(out=ptv, in0=ptv, in1=esl, op=ALU.mult),
                             reads=[Bpt[pti], Bet[hb]], writes=[Bpt[pti]])
                        S.op("dve", lambda e, pti=pti, v=v: e.tensor_tensor(out=pt[:, pti, :, :], in0=pt[:, pti, :, :],
                                                                           in1=amask[:, v, :, :], op=ALU.mult),
                             reads=[Bpt[pti], Bam], writes=[Bpt[pti]])
                        for c in range(8):
                            S.op("pe", lambda e, pbi=pbi, c=c, pti=pti, p0=p0, hh=hh: e.matmul(
                                banks[pbi][:, hh * 65:(hh + 1) * 65], lhsT=pt[:, pti, c, :],
                                rhs=va[:, hb, p0 + c, hh * 65:(hh + 1) * 65], start=(c == 0), stop=(c == 7)),
                                reads=[Bpt[pti], Bv[hb]], writes=[Bbank[pbi]], inc=(c == 7), partial=True)
                    norm_heads(banks[pbi][:, 0:130], 2, catb[:, cbi, :, :], recip, Brec, Bbank[pbi], Bcb[cbi], posb, Bposb)
                    tb = 6 + (blk % 2)
                    S.op("pe", lambda e, tb=tb, cbi=cbi: e.transpose(out=banks16[tb][:, 0:128],
                                                                     in_=catb[:, cbi, :, :].rearrange("p h d -> p (h d)"), identity=ident),
                         reads=[Bcb[cbi], Bconst], writes=[Bbank[tb]])
                    dst = cap(cthp[:, hb, :], (8 * R) * 64 + 16 * C, [[64, 8], [1, 16]])
                    evac_copy(dst, banks16[tb][:, 0:128].rearrange("p (i j) -> p i j", j=16), [Bbank[tb]], [Bct[hb]], partial=True)
                    blk += 1
            S.dma("sp", catT.ap()[hp], cthp[:, hb, :], reads=[Bct[hb]])
        S.barrier()
        A.release(m)

    def P2B(seq, layer):
        l2 = layer // 2
        m = A.mark()
        new_psum()
        qb = A.alloc([4, L], BF16)
        kb = A.alloc([L], BF16)
        vb = A.alloc([32, 130], BF16)
        pt = A.alloc([2, 3, 512], BF16)
        catb = A.alloc([2, 4, 64], BF16)
        ctb = A.alloc([4, L], BF16)
        bm = A.alloc([5, 128], BF16)
        esk = A.alloc([16], F32)
        recip = A.alloc([8], F32)
        posb = A.alloc([260], F32)
        Bposb = Buf()
        Bq, Bk, Bv, Bbm, Bes, Brec, Bct = (Buf() for _ in range(7))
        Bpt = [Buf(), Buf()]
        Bcb = [Buf(), Buf()]
        S.dma("sp", bm, c_bands.ap(), writes=[Bbm])
        S.dma("sp", esk, dap(sink, l2 * 16, [[0, 128], [1, 16]]), writes=[Bes])
        S.op("act", lambda e: e.activation(out=esk, in_=esk, func=AF.Exp), reads=[Bes], writes=[Bes])
        u = 0
        for gp in range(2):
            for j in range(4):
                S.dma("sp", qb[:, j, :], qk.ap()[16 + gp * 4 + j], writes=[Bq], partial=True)
            S.dma("sp", kb, qk.ap()[24 + gp], writes=[Bk])
            S.dma("sp", vb, vs.ap()[:, (16 + 2 * gp) * 65:(16 + 2 * gp + 2) * 65].rearrange("(t p) c -> p t c", p=128), writes=[Bv])
            for gl in range(2):
                g = 2 * gp + gl
                ps_ = slice(gl * 64, (gl + 1) * 64)
                for b in range(32):
                    chunks = [c for c in (b - 1, b, b + 1) if 0 <= c < 32]
                    pti = u % 2
                    sbanks = [0, 1, 2] if u % 2 == 0 else [3, 4, 5]
                    pbi = 6 + (u % 2)
                    u += 1
                    for ci, c in enumerate(chunks):
                        bi = sbanks[ci]
                        S.op("pe", lambda e, bi=bi, c=c, b=b, ps_=ps_: e.matmul(
                            banks[bi].rearrange("p (h q) -> p h q", q=128), lhsT=kb[ps_, c * 128:(c + 1) * 128],
                            rhs=qb[ps_, :, b * 128:(b + 1) * 128], start=True, stop=True),
                            reads=[Bq, Bk], writes=[Bbank[bi]])
                        S.op("act", lambda e, bi=bi, ci=ci, pti=pti: e.activation(out=pt[:, pti, ci, :], in_=banks[bi], func=AF.Exp,
                                                                               scale=0.125),
                             reads=[Bbank[bi]], writes=[Bpt[pti]], partial=True)
                        if c != b:
                            mi = 3 if c < b else 4
                            pv_ = pt[:, pti, ci, :].rearrange("p (h q) -> p h q", q=128)
                            S.op("dve", lambda e, pv_=pv_, mi=mi: e.tensor_tensor(
                                out=pv_, in0=pv_, in1=bm[:, mi, :].unsqueeze(1).to_broadcast([128, 4, 128]), op=ALU.mult),
                                reads=[Bpt[pti], Bbm], writes=[Bpt[pti]])
                    for j in range(4):
                        for ci, c in enumerate(chunks):
                            S.op("pe", lambda e, pbi=pbi, j=j, ci=ci, c=c, pti=pti, gl=gl: e.matmul(
                                banks[pbi][:, j * 65:(j + 1) * 65], lhsT=pt[:, pti, ci, j * 128:(j + 1) * 128],
                                rhs=vb[:, c, gl * 65:(gl + 1) * 65], start=(ci == 0), stop=(ci == len(chunks) - 1)),
                                reads=[Bpt[pti], Bv], writes=[Bbank[pbi]], inc=(ci == len(chunks) - 1 and j == 3), partial=True)
                    cbi = pti
                    norm_heads(banks[pbi][:, 0:260], 4, catb[:, cbi, :, :], recip, Brec, Bbank[pbi], Bcb[cbi], posb, Bposb,
                               add_ap=esk[:, 4 * g:4 * g + 4], extra_reads=[Bes])
                    tb = sbanks[0]
                    tv = banks16[tb][:, 0:256].rearrange("p (k q) -> p k q", q=128)
                    cb2 = catb[:, cbi, :, :].rearrange("p h d -> p (h d)")
                    for k in range(2):
                        S.op("pe", lambda e, tv=tv, k=k, cb2=cb2: e.transpose(out=tv[:, k, :], in_=cb2[:, k * 128:(k + 1) * 128],
                                                                              identity=ident),
                             reads=[Bcb[cbi], Bconst], writes=[Bbank[tb]], inc=(k == 1))
                    evac_copy(ctb[:, gl * 2:gl * 2 + 2, b * 128:(b + 1) * 128], tv, [Bbank[tb]], [Bct], partial=True)
            for k in range(4):
                S.dma("sp", catT.ap()[8 + gp * 4 + k], ctb[:, k, :], reads=[Bct])
        S.barrier()
        A.release(m)

    def P2C(seq, layer):
        m = A.mark()
        new_psum()
        qc = A.alloc([2, L], BF16)
        kc = A.alloc([2, L], BF16)
        vc = A.alloc([2, 32, 130], BF16)
        pt = A.alloc([2, 3, 128], BF16)
        osb = A.alloc([2, 32, 130], F32)
        bm = A.alloc([5, 128], BF16)
        Bbm = Buf()
        Bq = [Buf(), Buf()]
        Bk = [Buf(), Buf()]
        Bv = [Buf(), Buf()]
        Bpt = [Buf(), Buf()]
        Bo = [Buf(), Buf()]
        S.dma("sp", bm, c_bands.ap(), writes=[Bbm])
        it = 0
        u = 0
        for g, d in enumerate((1, 4, 16)):
            nmt = 32 // d
            for hp in range(8):
                hb = it % 2
                it += 1
                S.dma("sp", qc[:, hb, :], qk.ap()[g * 16 + hp], writes=[Bq[hb]])
                S.dma("sp", kc[:, hb, :], qk.ap()[g * 16 + 8 + hp], writes=[Bk[hb]])
                c0 = (g * 16 + 2 * hp) * 65
                for r in range(d):
                    S.dma("sp", vc[:, hb, r * nmt:(r + 1) * nmt, :],
                          dap(vs, r * 3120 + c0, [[d * 3120, 128], [128 * d * 3120, nmt], [1, 130]]), writes=[Bv[hb]], partial=True)
                q2 = qc[:, hb, :]
                k2 = kc[:, hb, :]
                for r in range(d):
                    for mt in range(nmt):
                        ptile = r * nmt + mt
                        chunks = [c for c in (mt - 1, mt, mt + 1) if 0 <= c < nmt]
                        pbi = 6 + (u % 2)
                        u2 = u
                        u += 1
                        for hh in range(2):
                            ps_ = slice(hh * 64, (hh + 1) * 64)
                            bi = (2 * u2 + hh) % 4
                            pti = (2 * u2 + hh) % 2
                            rhs = cap(q2[ps_, :], r + d * mt * 128, [[d, 128]])
                            for ci, c in enumerate(chunks):
                                lhsT = cap(k2[ps_, :], r + d * c * 128, [[d, 128]])
                                S.op("pe", lambda e, bi=bi, ci=ci, lhsT=lhsT, rhs=rhs: e.matmul(
                                    banks[bi][:, ci * 128:(ci + 1) * 128], lhsT=lhsT, rhs=rhs, start=True, stop=True),
                                    reads=[Bq[hb], Bk[hb]], writes=[Bbank[bi]], inc=(ci == len(chunks) - 1), partial=True)
                            n = len(chunks)
                            S.op("act", lambda e, bi=bi, n=n, pti=pti: e.activation(
                                out=pt[:, pti, 0:n, :], in_=banks[bi][:, 0:n * 128].rearrange("p (c q) -> p c q", q=128),
                                func=AF.Exp, scale=0.125), reads=[Bbank[bi]], writes=[Bpt[pti]])
                            m0 = chunks[0] - mt + 1
                            S.op("dve", lambda e, n=n, pti=pti, m0=m0: e.tensor_tensor(
                                out=pt[:, pti, 0:n, :], in0=pt[:, pti, 0:n, :], in1=bm[:, m0:m0 + n, :], op=ALU.mult),
                                reads=[Bpt[pti], Bbm], writes=[Bpt[pti]])
                            for ci, c in enumerate(chunks):
                                S.op("pe", lambda e, pbi=pbi, ci=ci, c=c, pti=pti, hh=hh, r=r: e.matmul(
                                    banks[pbi][:, hh * 65:(hh + 1) * 65], lhsT=pt[:, pti, ci, :],
                                    rhs=vc[:, hb, r * nmt + c, hh * 65:(hh + 1) * 65], start=(ci == 0), stop=(ci == n - 1)),
                                    reads=[Bpt[pti], Bv[hb]], writes=[Bbank[pbi]], inc=(ci == n - 1), partial=True)
                        evac_copy(osb[:, hb, ptile, :], banks[pbi][:, 0:130], [Bbank[pbi]], [Bo[hb]], partial=True, eng="act")
                for r in range(d):
                    S.dma("sp", dap(ocs, g * L * 1040 + r * 1040 + hp * 130, [[d * 1040, 128], [128 * d * 1040, nmt], [1, 130]]),
                          osb[:, hb, r * nmt:(r + 1) * nmt, :], reads=[Bo[hb]])
        S.barrier()
        A.release(m)
        m = A.mark()
        new_psum()
        og = A.alloc([2, 3, 1040], F32)
        osum = A.alloc([1040], F32)
        catc = A.alloc([1024], BF16)
        cstg = A.alloc([2, 8, 512], BF16)
        recip = A.alloc([16], F32)
        Bog = [Buf(), Buf()]
        Bos, Bcat, Brec = Buf(), Buf(), Buf()
        Bcs = [Buf(), Buf()]
        for t in range(32):
            ob = t % 2
            S.dma("sp", og[:, ob, :, :], ocs.ap()[:, t * 128:(t + 1) * 128, :].rearrange("g p c -> p g c"), writes=[Bog[ob]])
            S.op("dve", lambda e, ob=ob: e.tensor_tensor(out=osum, in0=og[:, ob, 0, :], in1=og[:, ob, 1, :], op=ALU.add),
                 reads=[Bog[ob]], writes=[Bos])
            S.op("dve", lambda e, ob=ob: e.tensor_tensor(out=osum, in0=osum, in1=og[:, ob, 2, :], op=ALU.add),
                 reads=[Bog[ob], Bos], writes=[Bos])
            norm_heads(osum, 16, catc.rearrange("p (h d) -> p h d", d=64), recip, Brec, Bos, Bcat, None, None)
            tb = t % 2
            tv = banks16[tb].rearrange("p (k q) -> p k q", q=128)
            for k in range(8):
                S.op("pe", lambda e, tv=tv, k=k: e.transpose(out=tv[:, k, :], in_=catc[:, k * 128:(k + 1) * 128], identity=ident),
                     reads=[Bcat, Bconst], writes=[Bbank[tb]], inc=(k == 7))
            sb = (t // 4) % 2
            evac_copy(cstg[:, sb, :, (t % 4) * 128:(t % 4 + 1) * 128], tv, [Bbank[tb]], [Bcs[sb]], partial=True)
            if t % 4 == 3:
                t0 = (t - 3) * 128
                S.dma("sp", catT.ap()[0:8, :, t0:t0 + 512].rearrange("c p t -> p c t"), cstg[:, sb, :, :], reads=[Bcs[sb]])
        S.barrier()
        A.release(m)

    def P3(seq, layer):
        even = layer % 2 == 0
        l2 = layer // 2
        EC = 16 if even else 8
        m = A.mark()
        new_psum()
        xt = A.alloc([4, D], F32)
        msb = A.alloc([2, D], F32)
        hn = A.alloc([D], BF16)
        hnT = A.alloc([16, T], BF16)
        ctT = A.alloc([EC, T], BF16)
        hid = A.alloc([NFC, T], BF16)
        gbm = A.alloc([D], F32)
        gbf = A.alloc([D], F32)
        sg = A.alloc([2, T], F32)
        ss = A.alloc([8], F32)
        sm = A.alloc([8], F32)
        Bxt = [Buf() for _ in range(4)]
        Bmsb = [Buf(), Buf()]
        Bhn, BhnT, Bct, Bhid, Bg, Bss, Brs, Bsm, Brm = (Buf() for _ in range(9))
        Bsg = [Buf(), Buf()]
        S.dma("sp", gbm, dap(g4, (1 * 4 + layer) * D, [[0, 128], [1, D]]), writes=[Bg], partial=True)
        S.dma("sp", gbf, dap(g4, (3 * 4 + layer) * D, [[0, 128], [1, D]]), writes=[Bg], partial=True)
        src = xin if layer == 0 else y
        cgs4 = [(0, 512), (512, 512), (1024, 512), (1536, 512)]

        def postnorm_residual(sp_, gb):
            rsm = sm[:, 4:8]
            for si in range(2):
                for ci in range(4):
                    bi = si * 4 + ci
                    evac_copy(msb[:, si, ci * 512:(ci + 1) * 512], banks[bi], [Bbank[bi]], [Bmsb[si]], partial=True, eng="act")
            S.op("dve", lambda e: e.memset(sm[:, 0:2], 0.0), writes=[Bsm])
            for si in range(2):
                S.op("act", lambda e, si=si: e.activation(out=hn, in_=msb[:, si, :], func=AF.Square, scale=D ** -0.5,
                                                          accum_out=sm[:, si:si + 1]),
                     reads=[Bmsb[si]], writes=[Bhn, Bsm])
            rstd_from_ss(sm, rsm, 2, Bsm, Brm)
            for si in range(2):
                st = 2 * sp_ + si
                S.op("dve", lambda e, si=si: e.scalar_tensor_tensor(out=msb[:, si, :], in0=msb[:, si, :], scalar=rsm[:, si:si + 1],
                                                                    in1=gb, op0=ALU.mult, op1=ALU.mult),
                     reads=[Bmsb[si], Brm, Bg], writes=[Bmsb[si]])
                S.op("dve", lambda e, si=si, st=st: e.tensor_tensor(out=xt[:, st, :], in0=xt[:, st, :], in1=msb[:, si, :], op=ALU.add),
                     reads=[Bmsb[si], Bxt[st]], writes=[Bxt[st]])

        for tt in range(NT):
            r0 = seq * L + tt * T
            S.dma("sp", ctT, catT.ap()[0:EC, :, tt * T:(tt + 1) * T].rearrange("c p t -> p c t"), writes=[Bct])
            for st in range(4):
                S.dma("sp", xt[:, st, :], src.ap()[r0 + st * 128:r0 + (st + 1) * 128, :], writes=[Bxt[st]])
            for sp_ in range(2):
                def wnext(k):
                    srcw = (wo_ab if even else wo_c).ap()[l2, k * 128:(k + 1) * 128, :]
                    return ring.next(srcw, 2048, ("wo", layer, k))
                lin_tok_pass(ctT, Bct, EC, [2 * sp_, 2 * sp_ + 1], cgs4, wnext, list(range(8)))
                postnorm_residual(sp_, gbm)
            prenorm_T(xt, Bxt, hn, Bhn, hnT, BhnT, ss, Bss, Brs, 2 * 4 + layer, [6, 7])
            for fc in range(NFC):
                bg = (fc % 2) * 2
                bu = bg + 1
                for which, bi in ((0, bg), (1, bu)):
                    slot, bsl = w_fm("wgu", layer, 2 * fc + which)
                    for dc in range(16):
                        S.op("pe", lambda e, bi=bi, dc=dc, slot=slot: e.matmul(banks[bi], lhsT=slot[:, dc, :], rhs=hnT[:, dc, :],
                                                                               start=(dc == 0), stop=(dc == 15)),
                             reads=[bsl, BhnT], writes=[Bbank[bi]], inc=(dc == 15))
                sgi = fc % 2
                S.op("act", lambda e, bg=bg, sgi=sgi: e.activation(out=sg[:, sgi, :], in_=banks[bg], func=AF.Silu),
                     reads=[Bbank[bg]], writes=[Bsg[sgi]])
                S.op("dve", lambda e, bu=bu, sgi=sgi, fc=fc: e.tensor_tensor(out=hid[:, fc, :], in0=banks[bu], in1=sg[:, sgi, :], op=ALU.mult),
                     reads=[Bbank[bu], Bsg[sgi]], writes=[Bhid], partial=True)
            for sp_ in range(2):
                def wnext(k):
                    return ring.next(wd.ap()[layer, k * 128:(k + 1) * 128, :], 2048, ("wd", layer, k))
                lin_tok_pass(hid, Bhid, NFC, [2 * sp_, 2 * sp_ + 1], cgs4, wnext, list(range(8)))
                postnorm_residual(sp_, gbf)
            for st in range(4):
                S.dma("sp", y.ap()[r0 + st * 128:r0 + (st + 1) * 128, :], xt[:, st, :], reads=[Bxt[st]])
        S.barrier()
        A.release(m)

    def program():
        ring.reset()
        for layer in range(nlayers):
            emit_casts(layer)
        S.barrier(dma_pools=("sp", "pool"))
        load_consts()
        if "E" in phases:
            build_etab(0)
            if nlayers > 2:
                build_etab(1)
        for seq in range(nseq):
            if seq > 0:
                S.rotate()
                ring.epoch = seq
            for layer in range(nlayers):
                if "z" in phases and layer >= 1:
                    continue
                if "1" in phases:
                    P1(seq, layer)
                if layer % 2 == 0:
                    if "A" in phases:
                        P2A(seq, layer)
                    if "B" in phases:
                        P2B(seq, layer)
                elif "C" in phases:
                    P2C(seq, layer)
                if "3" in phases:
                    P3(seq, layer)
                    if dbg and layer == 0 and seq == 0:
                        for i in range(8):
                            S.dma("sp", ydbg.ap()[i * 512:(i + 1) * 512, :], y.ap()[i * 512:(i + 1) * 512, :])
                        S.barrier()

    S.dry = True
    program()
    S.dry = False
    program()
    S.final_wait()
    S.flush()
    S.close()
    A.release(0)
    es.close()
    return nc, S


_CACHE = {}
NSEQ_PER_LAUNCH = 2


def _get_program():
    if "nc" not in _CACHE:
        _CACHE["nc"], _CACHE["S"] = build(nseq=NSEQ_PER_LAUNCH)
        _CACHE["consts"] = host_consts()
    return _CACHE["nc"], _CACHE["consts"]


def make_inmap(xs, inputs, consts, nc=None):
    g4 = np.concatenate([inputs["g_mix_pre"], inputs["g_mix_post"], inputs["g_ffn_pre"], inputs["g_ffn_post"]], axis=0)
    rpbp = np.zeros((2, 16, 31, 128), np.float32)
    rpbp[:, :, 8:23, 48:79] = inputs["rpb_a"]
    m = {
        "xin": xs, "g4": np.ascontiguousarray(g4, dtype=np.float32),
        "w_in_ab": inputs["w_in_ab"], "w_out_ab": inputs["w_out_ab"], "rpbp": rpbp, "sink": inputs["sink_b"],
        "w_in_c": inputs["w_in_c"], "w_out_c": inputs["w_out_c"], "w_gate": inputs["w_gate"], "w_up": inputs["w_up"],
        "w_down": inputs["w_down"],
    }
    m["gcolh"] = np.ascontiguousarray(m["g4"].reshape(16, 16, 128).transpose(2, 0, 1).reshape(128, 256))
    m.update(consts)
    if nc is not None:
        shapes = {}
        for alloc in nc.allocations:
            try:
                if alloc.kind == "ExternalInput":
                    shapes[alloc.memorylocations[0].name] = tuple(alloc.tensor_shape)
            except Exception:
                pass
        for k in list(m.keys()):
            if k in shapes and tuple(m[k].shape) != shapes[k]:
                m[k] = np.zeros(shapes[k], m[k].dtype)
    return m


def kernel(**inputs):
    inputs = {k: np.asarray(v) for k, v in inputs.items()}
    nc, consts = _get_program()
    xp = inputs["x_prompt"]
    xsm = inputs["x_sample"]
    if NSEQ_PER_LAUNCH == 2:
        in_maps = []
        for c in range(8):
            xs = np.concatenate([xp[c], xsm[c % 2]], axis=0).astype(np.float32, copy=False)
            in_maps.append(make_inmap(xs, inputs, consts))
        res = run_bass_kernel_spmd(nc, in_maps, core_ids=list(range(8)))
        yp = np.stack([res.results[c]["y"][:L] for c in range(8)], axis=0)
        ys = np.stack([res.results[c]["y"][L:] for c in range(2)], axis=0)
        return (yp.astype(np.float32), ys.astype(np.float32))
    in_maps = [make_inmap(np.ascontiguousarray(xp[c], dtype=np.float32), inputs, consts) for c in range(8)]
    res = run_bass_kernel_spmd(nc, in_maps, core_ids=list(range(8)))
    yp = np.stack([res.results[c]["y"] for c in range(8)], axis=0)
    in_maps = [make_inmap(np.ascontiguousarray(xsm[c % 2], dtype=np.float32), inputs, consts) for c in range(8)]
    res = run_bass_kernel_spmd(nc, in_maps, core_ids=list(range(8)))
    ys = np.stack([res.results[c]["y"] for c in range(2)], axis=0)
    return (yp.astype(np.float32), ys.astype(np.float32))
```
